# Optimizing a Trainium2 kernel written in Bass

```python
import jax, jax.numpy as jnp
from jax import lax
import numpy as np

D_MODEL = 2048
BATCH = 16
SEQ = 256
DEPTH = 1
DEC_BATCH = 2
DEC_SEQ = 1024
PAST_LEN = 256

GRID_W = 64
N_DIR = 2
A_W = D_MODEL // 2
B_W = D_MODEL - A_W
DK_A = 128
DV_A = 128
H_A = A_W // DV_A
DV_B = 256
DK_B = DV_B // 2
H_B = B_W // DV_B
CONV_W = 3
CHUNK = 64
FFN = 4 * D_MODEL
EPS = 1e-6
SPLIT_SIZES = (3 * A_W, A_W, N_DIR * H_A, N_DIR * H_A,
               H_B * DK_B, H_B * DK_B, B_W, B_W, N_DIR * H_B, N_DIR * H_B)
PROJ_W = sum(SPLIT_SIZES)

kernel_name = "hybrid_deltanet_mlstm_diffusion_step"


def rmsnorm(x, g):
    xf = x.astype(jnp.float32)
    y = xf * lax.rsqrt(jnp.mean(xf * xf, axis=-1, keepdims=True) + EPS)
    return y.astype(x.dtype) * g


def l2norm(x):
    return x * lax.rsqrt(jnp.sum(x * x, axis=-1, keepdims=True) + EPS)


def flip(t):
    return jnp.flip(t, axis=1)


def conv_centred(x, w):
    pad = CONV_W // 2
    t = x.shape[1]
    xp = jnp.pad(x, ((0, 0), (pad, pad), (0, 0)))
    return sum(xp[:, j:j + t] * w[:, j] for j in range(CONV_W))


def to_chunks(x):
    b, t = x.shape[:2]
    x = x.reshape(b, t // CHUNK, CHUNK, *x.shape[2:])
    return jnp.swapaxes(x, 2, 3)


def from_chunks(y):
    n, b, h, c, v = y.shape
    return jnp.transpose(y, (1, 0, 3, 2, 4)).reshape(b, n * c, h, v)


def gated_delta_chunked(q, k, v, g, beta, s0):
    dk = q.shape[-1]
    dv = v.shape[-1]
    qc = to_chunks(q) * dk ** -0.5
    kc = to_chunks(k)
    vc = to_chunks(v)
    bc = to_chunks(beta)
    gc = jnp.cumsum(to_chunks(g), axis=-1)
    incl = jnp.tril(jnp.ones((CHUNK, CHUNK), bool))
    strict = jnp.tril(jnp.ones((CHUNK, CHUNK), bool), -1)
    decay = jnp.exp(jnp.where(incl, gc[..., :, None] - gc[..., None, :], -jnp.inf))
    kb = kc * bc[..., None]
    lower = jnp.where(strict, jnp.einsum('bnhik,bnhjk->bnhij', kb, kc) * decay, 0.0)
    a_mat = lower + jnp.eye(CHUNK, dtype=lower.dtype)
    rhs = jnp.concatenate([vc * bc[..., None], kb * jnp.exp(gc)[..., None]], axis=-1)
    sol = lax.linalg.triangular_solve(a_mat, rhs, left_side=True, lower=True, unit_diagonal=True)
    u, w = sol[..., :dv], sol[..., dv:]
    attn = jnp.einsum('bnhik,bnhjk->bnhij', qc, kc) * decay
    qg = qc * jnp.exp(gc)[..., None]
    g_last = gc[..., -1]
    kd = kc * jnp.exp(g_last[..., None] - gc)[..., None]

    def step(s, xs):
        qg_i, kd_i, u_i, w_i, attn_i, gl_i = xs
        v_new = u_i - jnp.einsum('bhck,bhkv->bhcv', w_i, s)
        o = jnp.einsum('bhck,bhkv->bhcv', qg_i, s) + jnp.einsum('bhij,bhjv->bhiv', attn_i, v_new)
        s = s * jnp.exp(gl_i)[..., None, None] + jnp.einsum('bhck,bhcv->bhkv', kd_i, v_new)
        return s, o

    xs = tuple(jnp.moveaxis(t, 1, 0) for t in (qg, kd, u, w, attn, g_last))
    s_final, o = lax.scan(step, s0, xs)
    return from_chunks(o), s_final


def mlstm_chunked(q, k, v, log_i, log_f, c0, n0, m0):
    dk = q.shape[-1]
    qc = to_chunks(q) * dk ** -0.5
    kc = to_chunks(k)
    vc = to_chunks(v)
    ic = to_chunks(log_i)
    bc = jnp.cumsum(to_chunks(log_f), axis=-1)
    incl = jnp.tril(jnp.ones((CHUNK, CHUNK), bool))
    log_d = jnp.where(incl, bc[..., :, None] - bc[..., None, :] + ic[..., None, :], -jnp.inf)
    log_end = bc[..., -1:] - bc + ic
    qk = jnp.einsum('bnhik,bnhjk->bnhij', qc, kc)

    def step(carry, xs):
        c, n, m = carry
        q_i, k_i, v_i, b_i, ld_i, qk_i, le_i = xs
        inter = b_i + m[..., None]
        m_t = jnp.maximum(inter, jnp.max(ld_i, axis=-1))
        dw = jnp.exp(ld_i - m_t[..., None]) * qk_i
        iw = jnp.exp(inter - m_t)
        num = iw[..., None] * jnp.einsum('bhck,bhkv->bhcv', q_i, c) + jnp.einsum('bhij,bhjv->bhiv', dw, v_i)
        den = iw * jnp.einsum('bhck,bhk->bhc', q_i, n) + jnp.sum(dw, axis=-1)
        h = num / jnp.maximum(jnp.abs(den), jnp.exp(-m_t))[..., None]
        b_last = b_i[..., -1] + m
        m_new = jnp.maximum(b_last, jnp.max(le_i, axis=-1))
        ks = k_i * jnp.exp(le_i - m_new[..., None])[..., None]
        dec = jnp.exp(b_last - m_new)
        c = dec[..., None, None] * c + jnp.einsum('bhck,bhcv->bhkv', ks, v_i)
        n = dec[..., None] * n + jnp.sum(ks, axis=-2)
        return (c, n, m_new), h

    xs = tuple(jnp.moveaxis(t, 1, 0) for t in (qc, kc, vc, bc, log_d, qk, log_end))
    (c_f, n_f, m_f), h = lax.scan(step, (c0, n0, m0), xs)
    return from_chunks(h), c_f, n_f, m_f


def mixer(h, grid, init_states, w_in, conv_w, a_log, dt_bias, norm_a, ibias, fbias, norm_b, w_out):
    f32 = jnp.float32
    bn, t, _ = h.shape
    s0, c0, n0, m0 = (s.astype(f32) for s in init_states)
    proj = h @ w_in
    split_idx = np.cumsum(SPLIT_SIZES)[:-1].tolist()
    aqkv, ag, aa, ab, bq, bk, bv, bo, bi, bf = jnp.split(proj, split_idx, axis=-1)
    if grid:
        rows = t // GRID_W
        aqkv = conv_centred(aqkv.reshape(bn * rows, GRID_W, 3 * A_W), conv_w).reshape(bn, t, 3 * A_W)
    else:
        aqkv = conv_centred(aqkv, conv_w)
    aq, ak, av = jnp.split(jax.nn.silu(aqkv).astype(f32), 3, axis=-1)
    aq = l2norm(aq.reshape(bn, t, H_A, DK_A))
    ak = l2norm(ak.reshape(bn, t, H_A, DK_A))
    av = av.reshape(bn, t, H_A, DV_A)
    g = -jnp.exp(a_log.astype(f32)) * jax.nn.softplus(aa.reshape(bn, t, N_DIR, H_A).astype(f32) + dt_bias)
    beta = jax.nn.sigmoid(ab.reshape(bn, t, N_DIR, H_A).astype(f32))
    oa_f, sa_f = gated_delta_chunked(aq, ak, av, g[:, :, 0], beta[:, :, 0], s0[:, 0])
    oa_b, sa_b = gated_delta_chunked(flip(aq), flip(ak), flip(av), flip(g[:, :, 1]), flip(beta[:, :, 1]), s0[:, 1])
    ya = rmsnorm(oa_f + flip(oa_b), norm_a) * jax.nn.silu(ag.reshape(bn, t, H_A, DV_A).astype(f32))
    bq = bq.reshape(bn, t, H_B, DK_B).astype(f32)
    bk = bk.reshape(bn, t, H_B, DK_B).astype(f32)
    bv = bv.reshape(bn, t, H_B, DV_B).astype(f32)
    li = bi.reshape(bn, t, N_DIR, H_B).astype(f32) + ibias
    lf = jax.nn.log_sigmoid(bf.reshape(bn, t, N_DIR, H_B).astype(f32) + fbias)
    hb_f, cf, nf, mf = mlstm_chunked(bq, bk, bv, li[:, :, 0], lf[:, :, 0], c0[:, 0], n0[:, 0], m0[:, 0])
    hb_b, cb, nb, mb = mlstm_chunked(flip(bq), flip(bk), flip(bv), flip(li[:, :, 1]), flip(lf[:, :, 1]),
                                     c0[:, 1], n0[:, 1], m0[:, 1])
    yb = rmsnorm(hb_f + flip(hb_b), norm_b) * jax.nn.sigmoid(bo.reshape(bn, t, H_B, DV_B).astype(f32))
    y = jnp.concatenate([ya.reshape(bn, t, A_W), yb.reshape(bn, t, B_W)], axis=-1).astype(h.dtype) @ w_out
    states = (jnp.stack([sa_f, sa_b], axis=1), jnp.stack([cf, cb], axis=1),
              jnp.stack([nf, nb], axis=1), jnp.stack([mf, mb], axis=1))
    return y, states


def block(x, mod, init_states, lp, grid):
    (pre1, post1, pre2, post2, w_in, conv_w, a_log, dt_bias, norm_a, ibias, fbias, norm_b,
     w_out, w1, w2) = lp
    shift1, scale1, gate1, shift2, scale2, gate2 = jnp.split(mod, 6, axis=-1)
    h = rmsnorm(x, pre1) * (1.0 + scale1) + shift1
    mix, states = mixer(h, grid, init_states, w_in, conv_w, a_log, dt_bias, norm_a, ibias, fbias, norm_b, w_out)
    x = x + gate1 * rmsnorm(mix, post1)
    h = rmsnorm(x, pre2) * (1.0 + scale2) + shift2
    f = jnp.square(jax.nn.relu(h @ w1)) @ w2
    x = x + gate2 * rmsnorm(f, post2)
    return x, states


def setup_inputs(seed: int = 0) -> dict:
    key = jax.random.key(seed)
    ks = jax.random.split(key, 32)
    nrm = jax.random.normal
    d = D_MODEL
    dt = jnp.exp(jax.random.uniform(ks[17], (DEPTH, N_DIR, H_A), minval=np.log(1e-3), maxval=np.log(1e-1)))
    return {
        'x_prompt': nrm(ks[0], (BATCH, SEQ, d), jnp.float32),
        'x_sample': nrm(ks[1], (DEC_BATCH, DEC_SEQ, d), jnp.float32),
        'state_delta': 0.05 * nrm(ks[2], (DEC_BATCH, DEPTH, N_DIR, H_A, DK_A, DV_A), jnp.float32),
        'state_mlstm_C': 0.1 * nrm(ks[3], (DEC_BATCH, DEPTH, N_DIR, H_B, DK_B, DV_B), jnp.float32),
        'state_mlstm_n': 0.1 * nrm(ks[4], (DEC_BATCH, DEPTH, N_DIR, H_B, DK_B), jnp.float32),
        'state_mlstm_m': 0.5 * nrm(ks[5], (DEC_BATCH, DEPTH, N_DIR, H_B), jnp.float32),
        'c': nrm(ks[6], (DEC_BATCH, d), jnp.float32),
        'c_ctx': nrm(ks[7], (d,), jnp.float32),
        'w_ada': 0.5 * d ** -0.5 * nrm(ks[8], (DEPTH, d, 6 * d), jnp.float32),
        'b_ada': 0.01 * nrm(ks[9], (DEPTH, 6 * d), jnp.float32),
        'norm_mix_pre': 1.0 + 0.05 * nrm(ks[10], (DEPTH, d), jnp.float32),
        'norm_mix_post': 1.0 + 0.05 * nrm(ks[11], (DEPTH, d), jnp.float32),
        'norm_ffn_pre': 1.0 + 0.05 * nrm(ks[12], (DEPTH, d), jnp.float32),
        'norm_ffn_post': 1.0 + 0.05 * nrm(ks[13], (DEPTH, d), jnp.float32),
        'w_in': d ** -0.5 * nrm(ks[14], (DEPTH, d, PROJ_W), jnp.float32),
        'conv_w': CONV_W ** -0.5 * nrm(ks[15], (DEPTH, 3 * A_W, CONV_W), jnp.float32),
        'a_log': jnp.log(jax.random.uniform(ks[16], (DEPTH, N_DIR, H_A), minval=1.0, maxval=16.0)),
        'dt_bias': dt + jnp.log(-jnp.expm1(-dt)),
        'norm_a': 1.0 + 0.05 * nrm(ks[18], (DEPTH, DV_A), jnp.float32),
        'mlstm_ibias': -1.0 + 0.1 * nrm(ks[19], (DEPTH, N_DIR, H_B), jnp.float32),
        'mlstm_fbias': jax.random.uniform(ks[20], (DEPTH, N_DIR, H_B), minval=3.0, maxval=6.0),
        'norm_b': 1.0 + 0.05 * nrm(ks[21], (DEPTH, DV_B), jnp.float32),
        'w_out': d ** -0.5 * nrm(ks[22], (DEPTH, d, d), jnp.float32),
        'w_ffn1': d ** -0.5 * nrm(ks[23], (DEPTH, d, FFN), jnp.float32),
        'w_ffn2': FFN ** -0.5 * nrm(ks[24], (DEPTH, FFN, d), jnp.float32),
    }


def reference(x_prompt, x_sample, state_delta, state_mlstm_C, state_mlstm_n, state_mlstm_m, c, c_ctx,
              w_ada, b_ada, norm_mix_pre, norm_mix_post, norm_ffn_pre, norm_ffn_post, w_in, conv_w,
              a_log, dt_bias, norm_a, mlstm_ibias, mlstm_fbias, norm_b, w_out, w_ffn1, w_ffn2):
    f32 = jnp.float32
    n_ctx = x_prompt.shape[0]
    zero_states = (jnp.zeros((n_ctx, N_DIR, H_A, DK_A, DV_A), f32),
                   jnp.zeros((n_ctx, N_DIR, H_B, DK_B, DV_B), f32),
                   jnp.zeros((n_ctx, N_DIR, H_B, DK_B), f32),
                   jnp.zeros((n_ctx, N_DIR, H_B), f32))
    y_prompt, y_sample = x_prompt, x_sample
    ctx_states = ([], [], [], [])
    for l in range(DEPTH):
        lp = (norm_mix_pre[l], norm_mix_post[l], norm_ffn_pre[l], norm_ffn_post[l], w_in[l], conv_w[l],
              a_log[l], dt_bias[l], norm_a[l], mlstm_ibias[l], mlstm_fbias[l], norm_b[l],
              w_out[l], w_ffn1[l], w_ffn2[l])
        mod_ctx = (jax.nn.silu(c_ctx) @ w_ada[l] + b_ada[l])[None, None, :]
        mod_lat = (jax.nn.silu(c) @ w_ada[l] + b_ada[l])[:, None, :]
        y_prompt, st = block(y_prompt, mod_ctx, zero_states, lp, False)
        for acc, s in zip(ctx_states, st):
            acc.append(s.astype(x_prompt.dtype))
        cached = (state_delta[:, l], state_mlstm_C[:, l], state_mlstm_n[:, l], state_mlstm_m[:, l])
        y_sample, _ = block(y_sample, mod_lat, cached, lp, True)
    new_state_delta = jnp.stack(ctx_states[0], axis=1)
    new_state_mlstm_C = jnp.stack(ctx_states[1], axis=1)
    new_state_mlstm_n = jnp.stack(ctx_states[2], axis=1)
    new_state_mlstm_m = jnp.stack(ctx_states[3], axis=1)
    return (y_prompt, y_sample, new_state_delta, new_state_mlstm_C, new_state_mlstm_n, new_state_mlstm_m)
```

```python
import numpy as np
import concourse.bass as bass
import concourse.mybir as mybir
from concourse.bass_utils import run_bass_kernel_spmd

F32 = mybir.dt.float32
BF16 = mybir.dt.bfloat16
F32R = mybir.dt.float32r
AF = mybir.ActivationFunctionType
ALU = mybir.AluOpType
AX = mybir.AxisListType

ENGS = ('pe', 'act', 'dve', 'pool', 'sp')
SAME_ENGINE_SYNC = {'pe': False, 'act': True, 'dve': True, 'pool': True, 'sp': False}


class Tile:
    def __init__(self, fw, name, h):
        self.fw = fw
        self.name = name
        self.h = h
        self.last_writer = None
        self.readers = {}
        self.dma_sem = None
        self.dma_count = 0
        self.psum = False

    def __getitem__(self, k):
        return self.h[k]

    def ap(self):
        return self.h[:]


class SubTile:
    def __init__(self, parent, ap, name):
        self.__dict__['p'] = parent
        self.__dict__['apv'] = ap
        self.__dict__['name'] = name

    def __getitem__(self, k):
        return self.apv[k]

    def __getattr__(self, k):
        return getattr(self.p, k)

    def __setattr__(self, k, v):
        setattr(self.p, k, v)


class Op:
    __slots__ = ('eng', 'fn', 'deps', 'is_dma', 'sem', 'semval', 'needs_inc', 'tile')

    def __init__(self, eng, fn, is_dma=False):
        self.eng = eng
        self.fn = fn
        self.deps = []
        self.is_dma = is_dma
        self.sem = None
        self.semval = None
        self.needs_inc = False
        self.tile = None


class FW:
    def __init__(self, nc):
        self.nc = nc
        self.ops = []
        self.tiles = []
        self.nsb = 0
        self.scopes = []
        self.debug = False
        self.names = {}
        self.pending_join = None
        self.jt = None

    def sb(self, name, shape, dtype):
        h = self.nc.alloc_sbuf_tensor(name, list(shape), dtype)
        t = Tile(self, name, h)
        t.last_writer = self.pending_join
        self.tiles.append(t)
        return t

    def ps(self, name, shape, dtype=F32):
        h = self.nc.alloc_psum_tensor(name, list(shape), dtype)
        t = Tile(self, name, h)
        t.psum = True
        self.tiles.append(t)
        return t

    def scope_begin(self):
        self.scopes.append([])

    def sbs(self, name, shape, dtype):
        g = self.nc.sbuf_tensor(name, list(shape), dtype)
        h = g.__enter__()
        t = Tile(self, name, h)
        t.last_writer = self.pending_join
        self.tiles.append(t)
        self.scopes[-1].append((g, t))
        return t

    def scope_end(self):
        items = self.scopes.pop()
        jt = self.jt
        j = self.dma('sp', jt[0:1, 1:2], jt[0:1, 0:1], r=[], w=[jt] + [t for g, t in items])
        self.pending_join = j
        for g, t in reversed(items):
            g.__exit__(None, None, None)

    def view(self, name, h):
        t = Tile(self, name, h)
        self.tiles.append(t)
        return t

    def _track(self, o, r, w):
        deps = []
        for t in r:
            if t.last_writer is not None:
                deps.append(t.last_writer)
            if t.psum:
                deps.extend(rd for rd in t.readers.values() if rd.eng != o.eng)
        for t in w:
            if t.last_writer is not None:
                deps.append(t.last_writer)
            deps.extend(t.readers.values())
        seen = set()
        for d in deps:
            if d is o or id(d) in seen:
                continue
            seen.add(id(d))
            if (not d.is_dma) and d.eng == o.eng and not SAME_ENGINE_SYNC[o.eng]:
                continue
            o.deps.append(d)
            d.needs_inc = True
        for t in r:
            t.readers[id(o) if o.is_dma else o.eng] = o
        for t in w:
            t.last_writer = o
            t.readers = {}

    def op(self, eng, fn, r=(), w=()):
        o = Op(eng, fn)
        if self.debug:
            import sys as _s
            f = _s._getframe(1)
            while f is not None and f.f_code.co_name != 'build':
                f = f.f_back
            o.tile = f.f_lineno if f is not None else None
        self._track(o, r, w)
        self.ops.append(o)
        return o

    def dma(self, q, out, in_, r=(), w=(), group=None, **kw):
        o = Op(q, lambda e: e.dma_start(out=out, in_=in_, **kw), is_dma=True)
        o.tile = (list(w) + list(r))[0]
        if group is not None:
            o.deps = list(group.deps)
            for t in w:
                t.last_writer = o
        else:
            self._track(o, r, w)
        o.needs_inc = True
        self.ops.append(o)
        return o

    def emit(self):
        nc = self.nc
        sems = {e: nc.alloc_semaphore(name="s_" + e) for e in ENGS}
        cnt = {e: 0 for e in ENGS}
        for o in self.ops:
            if o.is_dma:
                t = o.tile
                if t.dma_sem is None:
                    t.dma_sem = nc.alloc_semaphore(name="d_" + t.name)
                t.dma_count += 1
                o.sem = t.dma_sem
                o.semval = 16 * t.dma_count
            elif o.needs_inc:
                cnt[o.eng] += 1
                o.sem = sems[o.eng]
                o.semval = cnt[o.eng]
        streams = {e: [o for o in self.ops if o.eng == e] for e in ENGS}
        finals = [(t.dma_sem, 16 * t.dma_count) for t in self.tiles if t.dma_sem is not None]
        self.n_instr = {e: len(streams[e]) for e in ENGS}

        def run(e, eng):
            waited = {}
            for o in streams[e]:
                need = {}
                for d in o.deps:
                    k = id(d.sem)
                    if waited.get(k, (None, 0))[1] >= d.semval:
                        continue
                    if k not in need or need[k][1] < d.semval:
                        need[k] = (d.sem, d.semval)
                for k, (s, v) in need.items():
                    eng.wait_ge(s, v)
                    waited[k] = (s, v)
                ins = o.fn(eng)
                if self.debug:
                    try:
                        self.names[ins.ins.name] = o.tile
                    except Exception:
                        pass
                if o.needs_inc:
                    if o.is_dma:
                        ins.then_inc(o.sem, 16)
                    else:
                        ins.then_inc(o.sem, 1)
            if e == 'sp':
                for s, v in finals:
                    eng.wait_ge(s, v)

        with nc.Block() as block:
            @block.sync
            def _(eng):
                run('sp', eng)

            @block.scalar
            def _(eng):
                run('act', eng)

            @block.vector
            def _(eng):
                run('dve', eng)

            @block.gpsimd
            def _(eng):
                run('pool', eng)

            @block.tensor
            def _(eng):
                run('pe', eng)


D = 2048
KC = 16
NT = 1024
NTILE = 8
NSLOT = 4
FFN = 8192
PW = 7216
EPS = 1e-6
BIG = 30000.0
NCORES = 8


class RP:
    def __init__(self, fw, name, shape, dtype, n, psum=False, scoped=False):
        mk = fw.ps if psum else (fw.sbs if scoped else fw.sb)
        self.t = [mk("%s%d" % (name, i), shape, dtype) for i in range(n)]
        self.i = 0

    def get(self):
        t = self.t[self.i % len(self.t)]
        self.i += 1
        return t


def make_consts():
    p = np.arange(128)
    same = (p[:, None] // 64) == (p[None, :] // 64)
    c = {}
    c['ident'] = np.eye(128, dtype=np.float32)
    c['trif'] = (same & (p[:, None] <= p[None, :])).astype(np.float32)
    c['trib'] = (same & (p[:, None] >= p[None, :])).astype(np.float32)
    c['onesl'] = np.broadcast_to((p[:, None] < 64), (128, 128)).astype(np.float32)
    c['onesr'] = np.broadcast_to((p[:, None] >= 64), (128, 128)).astype(np.float32)
    ustrict = same & (p[None, :] > p[:, None])
    lstrict = same & (p[None, :] < p[:, None])
    uincl = same & (p[None, :] >= p[:, None])
    lincl = same & (p[None, :] <= p[:, None])
    c['maskf'] = np.concatenate([-BIG * (~ustrict), BIG * (~lstrict), -BIG * (~uincl)], axis=1).astype(np.float32)
    c['maskb'] = np.concatenate([-BIG * (~lstrict), BIG * (~ustrict), -BIG * (~lincl)], axis=1).astype(np.float32)
    c['m01f'] = uincl.astype(np.float32)
    c['m01b'] = lincl.astype(np.float32)
    names = ['ident', 'trif', 'trib', 'onesl', 'onesr', 'maskf', 'maskb', 'm01f', 'm01b']
    cat = np.concatenate([c[n] for n in names], axis=1)
    offs = {}
    o = 0
    for n in names:
        offs[n] = (o, c[n].shape[1])
        o += c[n].shape[1]
    sel = np.zeros((16, 16, 128), np.float32)
    for s in range(16):
        sel[s, s, :] = 1.0
    return np.ascontiguousarray(cat), offs, sel.reshape(16, 16 * 128)


def build(stage=99):
    nc = bass.Bass("TRN2", target_bir_lowering=False)
    fw = FW(nc)
    cat, offs, selnp = make_consts()
    NCF = cat.shape[1]

    def din(name, shape):
        return nc.dram_tensor(name, list(shape), F32, kind="ExternalInput").ap()

    def dout(name, shape):
        return nc.dram_tensor(name, list(shape), F32, kind="ExternalOutput").ap()

    xin = din("xin", [NT, D])
    cond = din("cond", [1, D])
    flags = din("flags", [128, 2])
    sdelta = din("sdelta", [2, 8, 128, 128])
    sC = din("sC", [2, 4, 128, 256])
    sn = din("sn", [2, 4, 128])
    sm = din("sm", [1, 8])
    w_ada = din("w_ada", [D, 6 * D])
    b_ada = din("b_ada", [1, 6 * D])
    nrm = din("nrm", [4, D])
    w_in = din("w_in", [D, PW])
    conv_w = din("conv_w", [3072, 3])
    gparams = din("gparams", [1, 48])
    norm_a = din("norm_a", [1, 128])
    norm_b = din("norm_b", [1, 256])
    w_out = din("w_out", [D, D])
    w1 = din("w1", [D, FFN])
    w2 = din("w2", [FFN, D])
    cst = din("cst", [128, NCF])
    y = dout("y", [NT, D])
    st_delta = dout("st_delta", [4, 2, 8, 128, 128])
    st_C = dout("st_C", [4, 2, 4, 128, 256])
    st_n = dout("st_n", [4, 2, 4, 128])
    st_m = dout("st_m", [4, 8])

    dbgt = {}

    def act(out, in_, func, r, w, bias=None, scale=None, accum=None):
        kw = {}
        if bias is not None:
            kw['bias'] = bias
        if scale is not None:
            kw['scale'] = scale
        if accum is not None:
            kw['accum_out'] = accum
        return fw.op('act', lambda e: e.activation(out, in_, func, **kw), r=r, w=w)

    def tt(eng, out, a, b, op, r, w):
        return fw.op(eng, lambda e: e.tensor_tensor(out, a, b, op), r=r, w=w)

    def ts(eng, out, a, s1, s2, op0, op1, r, w):
        if op1 is None:
            return fw.op(eng, lambda e: e.tensor_scalar(out, a, s1, None, op0), r=r, w=w)
        return fw.op(eng, lambda e: e.tensor_scalar(out, a, s1, s2, op0, op1), r=r, w=w)

    def stt(eng, out, a, s, b, op0, op1, r, w):
        return fw.op(eng, lambda e: e.scalar_tensor_tensor(out, a, s, b, op0, op1), r=r, w=w)

    def cp(eng, out, in_, r, w):
        if eng == 'act':
            return fw.op('act', lambda e: e.activation(out, in_, AF.Copy), r=r, w=w)
        return fw.op(eng, lambda e: e.tensor_copy(out, in_), r=r, w=w)

    def mm(out, lhsT, rhs, start, stop, r, w):
        return fw.op('pe', lambda e: e.matmul(out, lhsT, rhs, start=start, stop=stop), r=r, w=w)

    def tr(out, in_, ident, r, w):
        return fw.op('pe', lambda e: e.transpose(out, in_, ident), r=r, w=w)

    def memset(eng, ap, v, w):
        return fw.op(eng, lambda e: e.memset(ap, v), r=[], w=w)

    def recip(out, in_, r, w):
        return fw.op('dve', lambda e: e.reciprocal(out, in_), r=r, w=w)

    fw.jt = fw.sb("jt", [1, 2], F32)
    fw.op('pool', lambda e: e.memset(fw.jt[0:1, 0:2], 0.0), r=[], w=[fw.jt])
    CF = fw.sb("cf", [128, NCF], F32)
    fw.dma('sp', CF[:], cst, w=[CF])

    def cf(name):
        o, n = offs[name]
        return CF[:, o:o + n]


    ONESB = fw.sb("onesb", [128, 128], BF16)
    memset('pool', ONESB[:], 1.0, [ONESB])
    ONE11 = fw.sb("one11", [1, 1], F32)
    memset('pool', ONE11[:], 1.0, [ONE11])
    EPSC = fw.sb("epsc", [128, 2], F32)
    memset('pool', EPSC[:, 0:1], EPS, [EPSC])
    memset('pool', EPSC[:, 1:2], 1.0, [EPSC])
    FLG = fw.sb("flg", [128, 2], F32)
    fw.dma('sp', FLG[:], flags, w=[FLG])

    PBIG = RP(fw, "pbig", [128, 512], F32, 3, psum=True)
    PN = fw.ps("pnorm", [128, 512], F32)
    _banks = [fw.ps("pbank%d" % i, [128, 512], F32) for i in range(4)]
    _subs = [SubTile(_banks[i], _banks[i][:, j * 128:(j + 1) * 128], "psm%d_%d" % (i, j))
             for j in range(4) for i in range(4)]
    PSM = RP.__new__(RP)
    PSM.t = _subs[:15]
    PSM.i = 0
    MODPS = _subs[15]

    WB = RP(fw, "wb", [128, 4096], BF16, 2)
    yT = fw.sb("yT", [128, KC, NT], BF16)
    modT = fw.sb("modT", [128, 96], F32)
    a1 = fw.sb("a1", [128, KC], F32)
    a2 = fw.sb("a2", [128, KC], F32)
    g1p = fw.sb("g1p", [128, KC], F32)
    g2p = fw.sb("g2p", [128, KC], F32)
    NAB = fw.sb("nab", [128, 3], F32)
    STG = RP(fw, "stg", [128, 257], F32, 2)

    def load_w(src, ncol, nk=KC):
        wb = WB.get()
        wv = wb[:, 0:nk * ncol].rearrange("p (k n) -> p k n", k=nk)
        sv = src.rearrange("(k p) n -> p k n", p=128)
        step = max(1, 512 // ncol) if ncol >= 128 else 4
        step = 4
        g = None
        for k4 in range(0, nk, step):
            g2 = fw.dma('pool', wv[:, k4:k4 + step, :], sv[:, k4:k4 + step, :], w=[wb], group=g)
            g = g or g2
        return wb, wv

    fw.scope_begin()
    hT = fw.sbs("hT", [128, KC, NT], BF16)
    condT = fw.sbs("condT", [128, KC], F32)
    badaT = fw.sbs("badaT", [128, 96], F32)
    nrmT = fw.sbs("nrmT", [128, 4, KC], F32)
    rows = fw.sbs("rows", [96, 3, 128], F32)
    sT = fw.sbs("sT", [128, KC], BF16)
    ROWT = [fw.sbs("rowt%d" % i, [1, 256], F32) for i in range(2)]
    fw.dma('sp', rows[0:16, 0, :], cond.rearrange("o (c p) -> (o c) p", p=128), w=[rows])
    fw.dma('sp', rows[0:96, 1, :], b_ada.rearrange("o (c p) -> (o c) p", p=128), w=[rows])
    fw.dma('sp', rows[0:64, 2, :], nrm.rearrange("o (c p) -> (o c) p", p=128), w=[rows])
    for (n, idx, dst) in ((16, 0, condT[:]), (96, 1, badaT[:]), (64, 2, nrmT[:].rearrange("p a c -> p (a c)"))):
        ps = PSM.get()
        tr(ps[:, 0:n], rows[0:n, idx, :], cf('ident')[0:n, 0:n], [rows, CF], [ps])
        cp('dve', dst, ps[:, 0:n], [ps], [condT, badaT, nrmT])
    act(sT[:], condT[:], AF.Silu, [condT], [sT])
    NR = fw.sbs("nr", [3, 128], F32)
    fw.dma('sp', NR[0:1, :], norm_a, w=[NR])
    fw.dma('sp', NR[1:3, :], norm_b.rearrange("o (c p) -> (o c) p", p=128), w=[NR])
    ps = PSM.get()
    tr(ps[:, 0:3], NR[0:3, :], cf('ident')[0:3, 0:3], [NR, CF], [ps])
    cp('dve', NAB[:], ps[:, 0:3], [ps], [NAB])

    rowi = [0]

    def mod_block(nb):
        wb, wv = load_w(w_ada[:, nb * 256:(nb + 1) * 256], 256)
        pb = PBIG.get()
        for kc in range(KC):
            mm(pb[0:1, 0:256], sT[:, kc:kc + 1], wv[:, kc, :], kc == 0, kc == KC - 1, [sT, wb], [pb])
        rt = ROWT[rowi[0] % 2]
        rowi[0] += 1
        cp('act', rt[:], pb[0:1, 0:256], [pb], [rt])
        for j in range(2):
            c = nb * 2 + j
            mm(MODPS[:, c:c + 1], rt[0:1, j * 128:(j + 1) * 128], ONE11[0:1, 0:1], True, True, [rt, ONE11], [MODPS])

    fw.scope_begin()
    CB = fw.sbs("cbf", [128, NCF], BF16)
    fw.dma('pool', CB[:], cst, w=[CB])
    XS = fw.sbs("xs", [128, D], F32)
    XN = fw.sbs("xn", [128, D], BF16)
    SSQ = [fw.sbs("ssq%d" % i, [128, 2], F32) for i in range(2)]
    for t in range(NTILE):
        xs = XS
        fw.dma('sp', xs[:], xin[t * 128:(t + 1) * 128, :], w=[xs])
        xn = XN
        sq = SSQ[t % 2]
        act(xn[:], xs[:], AF.Square, [xs], [xn, sq], accum=sq[:, 0:1])
        act(sq[:, 1:2], sq[:, 0:1], AF.Ln, [sq, EPSC], [sq], scale=1.0 / D, bias=EPSC[:, 0:1])
        act(sq[:, 1:2], sq[:, 1:2], AF.Exp, [sq], [sq], scale=-0.5)
        act(xn[:], xs[:], AF.Copy, [xs, sq], [xn], scale=sq[:, 1:2])
        for c in range(KC):
            ps = PSM.get()
            pv = ps[:].bitcast(BF16)
            tr(pv[:, 0:128], xn[:, c * 128:(c + 1) * 128], CB[:, offs['ident'][0]:offs['ident'][0] + 128], [xn, CB], [ps])
            cp('act' if c % 2 == 0 else 'dve', hT[:, c, t * 128:(t + 1) * 128], pv[:, 0:128], [ps], [hT])
    fw.scope_end()
    for nb in range(16):
        mod_block(nb)
    tt('dve', modT[:, 0:32], MODPS[:, 0:32], badaT[:, 0:32], ALU.add, [MODPS, badaT], [modT])
    stt('dve', a1[:], modT[:, 16:32], 1.0, nrmT[:, 0, :], ALU.add, ALU.mult, [modT, nrmT], [a1])

    for c in range(KC):
        if c % 2 == 0:
            act(hT[:, c, :], hT[:, c, :], AF.Identity, [hT, a1, modT], [hT], scale=a1[:, c:c + 1], bias=modT[:, c:c + 1])
        else:
            ts('dve', hT[:, c, :], hT[:, c, :], a1[:, c:c + 1], modT[:, c:c + 1], ALU.mult, ALU.add,
               [hT, a1, modT], [hT])

    def mod_finish():
        tt('dve', modT[:, 32:96], MODPS[:, 32:96], badaT[:, 32:96], ALU.add, [MODPS, badaT], [modT])
        tt('dve', g1p[:], modT[:, 32:48], nrmT[:, 1, :], ALU.mult, [modT, nrmT], [g1p])
        stt('dve', a2[:], modT[:, 64:80], 1.0, nrmT[:, 2, :], ALU.add, ALU.mult, [modT, nrmT], [a2])
        tt('dve', g2p[:], modT[:, 80:96], nrmT[:, 3, :], ALU.mult, [modT, nrmT], [g2p])

    GW = fw.sbs("gw", [128, KC, 48], BF16)
    for k4 in range(0, KC, 4):
        fw.dma('pool', GW[:, k4:k4 + 4, 0:32],
               w_in[:, 4096:4128].rearrange("(k p) n -> p k n", p=128)[:, k4:k4 + 4, :], w=[GW])
        fw.dma('pool', GW[:, k4:k4 + 4, 32:48],
               w_in[:, 7200:7216].rearrange("(k p) n -> p k n", p=128)[:, k4:k4 + 4, :], w=[GW])
    GPR = fw.sbs("gpr", [128, 48], F32)
    fw.dma('sp', GPR[:], gparams.partition_broadcast(128), w=[GPR])
    negA = fw.sbs("negA", [128, 16], F32)
    act(negA[:], GPR[:, 0:16], AF.Exp, [GPR], [negA])
    ts('dve', negA[:], negA[:], -1.0, None, ALU.mult, None, [negA], [negA])
    SM = fw.sbs("smt", [128, 8], F32)
    fw.dma('sp', SM[:], sm.partition_broadcast(128), w=[SM])
    EM0 = fw.sbs("em0", [128, 8], F32)
    act(EM0[:], SM[:], AF.Exp, [SM], [EM0])
    CWR = fw.sbs("cwr", [24, 384], F32)
    fw.dma('sp', CWR[:], conv_w.rearrange("(c p) j -> c (p j)", p=128), w=[CWR])
    CW = fw.sbs("cw", [128, 3, 24], F32)
    CWF = fw.sbs("cwf", [128, 2, 24], F32)
    cwr3 = CWR[:].rearrange("c (p j) -> c j p", j=3)
    for j in range(3):
        ps = PSM.get()
        tr(ps[:, 0:24], cwr3[0:24, j, :], cf('ident')[0:24, 0:24], [CWR, CF], [ps])
        cp('dve', CW[:, j, :], ps[:, 0:24], [ps], [CW])
    NFL = fw.sbs("nfl", [128, 1], F32)
    ts('dve', NFL[:], FLG[:, 0:1], -1.0, None, ALU.mult, None, [FLG], [NFL])
    ts('dve', CWF[:, 0, :], CW[:, 0, :], NFL[:, 0:1], None, ALU.mult, None, [CW, NFL], [CWF])
    ts('dve', CWF[:, 1, :], CW[:, 2, :], NFL[:, 0:1], None, ALU.mult, None, [CW, NFL], [CWF])

    GX = fw.sbs("gx", [128, NTILE, 64], F32)
    memset('pool', GX[:], 0.0, [GX])
    LB = fw.sbs("lb", [128, NTILE, 16], F32)
    CS = fw.sbs("cs", [128, NTILE, 24], F32)
    TOT = fw.sbs("tot", [128, NTILE, 2, 24], F32)
    EG = fw.sbs("eg", [128, NTILE, 2, 24], F32)
    DS = fw.sbs("ds", [128, NTILE, 104], F32)
    FMF = fw.sbs("fmf", [16, NT], F32)
    FMB = fw.sbs("fmb", [16, NT], F32)
    TMPG = [fw.sbs("tmpg%d" % i, [128, 48], F32) for i in range(2)]
    one_b = EPSC[:, 1:2]
    for t in range(NTILE):
        gp = PBIG.get()
        for kc in range(KC):
            mm(gp[:, 0:48], hT[:, kc, t * 128:(t + 1) * 128], GW[:, kc, :], kc == 0, kc == KC - 1, [hT, GW], [gp])
        tg = TMPG[t % 2]
        tt('dve', tg[:, 0:16], gp[:, 0:16], GPR[:, 16:32], ALU.add, [gp, GPR], [tg])
        act(tg[:, 0:16], tg[:, 0:16], AF.Exp, [tg], [tg])
        act(tg[:, 0:16], tg[:, 0:16], AF.Ln, [tg, EPSC], [tg], bias=one_b)
        act(tg[:, 16:32], gp[:, 16:32], AF.Exp, [gp], [tg], scale=-1.0)
        act(tg[:, 16:32], tg[:, 16:32], AF.Ln, [tg, EPSC], [tg], bias=one_b)
        tt('dve', tg[:, 40:48], gp[:, 40:48], GPR[:, 40:48], ALU.add, [gp, GPR], [tg])
        act(tg[:, 40:48], tg[:, 40:48], AF.Exp, [tg], [tg], scale=-1.0)
        act(tg[:, 40:48], tg[:, 40:48], AF.Ln, [tg, EPSC], [tg], bias=one_b)
        tt('dve', LB[:, t, 8:16], gp[:, 32:40], GPR[:, 32:40], ALU.add, [gp, GPR], [LB])
        tt('dve', GX[:, t, 0:8], tg[:, 0:8], negA[:, 0:8], ALU.mult, [tg, negA], [GX])
        tt('dve', GX[:, t, 8:16], tg[:, 0:8], negA[:, 0:8], ALU.mult, [tg, negA], [GX])
        tt('dve', GX[:, t, 32:40], tg[:, 8:16], negA[:, 8:16], ALU.mult, [tg, negA], [GX])
        tt('dve', GX[:, t, 40:48], tg[:, 8:16], negA[:, 8:16], ALU.mult, [tg, negA], [GX])
        ts('dve', GX[:, t, 24:32], tg[:, 16:24], -1.0, None, ALU.mult, None, [tg], [GX])
        ts('dve', GX[:, t, 56:64], tg[:, 24:32], -1.0, None, ALU.mult, None, [tg], [GX])
        ts('dve', LB[:, t, 0:8], tg[:, 40:48], -1.0, None, ALU.mult, None, [tg], [LB])
        cps = PSM.get()
        mm(cps[:, 0:8], cf('trif'), GX[:, t, 0:8], True, True, [CF, GX], [cps])
        mm(cps[:, 8:16], cf('trib'), GX[:, t, 32:40], True, True, [CF, GX], [cps])
        mm(cps[:, 16:20], cf('trif'), LB[:, t, 0:4], True, True, [CF, LB], [cps])
        mm(cps[:, 20:24], cf('trib'), LB[:, t, 4:8], True, True, [CF, LB], [cps])
        cp('act', CS[:, t, :], cps[:, 0:24], [cps], [CS])
        tps = PSM.get()
        for c, on in enumerate(('onesl', 'onesr')):
            b0 = c * 24
            mm(tps[:, b0:b0 + 8], cf(on), GX[:, t, 0:8], True, True, [CF, GX], [tps])
            mm(tps[:, b0 + 8:b0 + 16], cf(on), GX[:, t, 32:40], True, True, [CF, GX], [tps])
            mm(tps[:, b0 + 16:b0 + 24], cf(on), LB[:, t, 0:8], True, True, [CF, LB], [tps])
        cp('dve', TOT[:, t, :, :].rearrange("p c n -> p (c n)"), tps[:, 0:48], [tps], [TOT])
        act(EG[:, t, :, :].rearrange("p c n -> p (c n)"), tps[:, 0:48], AF.Exp, [tps], [EG])
        fps = PSM.get()
        mm(fps[0:16, :], GX[:, t, 0:16], cf('trif'), True, False, [GX, CF], [fps])
        mm(fps[0:16, :], GX[:, t, 16:32], cf('ident'), False, True, [GX, CF], [fps])
        cp('dve', FMF[0:16, t * 128:(t + 1) * 128], fps[0:16, :], [fps], [FMF])
        fps = PSM.get()
        mm(fps[0:16, :], GX[:, t, 32:48], cf('trib'), True, False, [GX, CF], [fps])
        mm(fps[0:16, :], GX[:, t, 48:64], cf('ident'), False, True, [GX, CF], [fps])
        cp('act', FMB[0:16, t * 128:(t + 1) * 128], fps[0:16, :], [fps], [FMB])
        ts('dve', DS[:, t, 0:16], CS[:, t, 0:16], -1.0, None, ALU.mult, None, [CS], [DS])
        tt('dve', DS[:, t, 16:24], CS[:, t, 0:8], GX[:, t, 24:32], ALU.add, [CS, GX], [DS])
        tt('dve', DS[:, t, 24:32], CS[:, t, 8:16], GX[:, t, 56:64], ALU.add, [CS, GX], [DS])
        act(DS[:, t, 32:40], GX[:, t, 24:32], AF.Exp, [GX], [DS])
        act(DS[:, t, 40:48], GX[:, t, 56:64], AF.Exp, [GX], [DS])
        act(DS[:, t, 48:64], DS[:, t, 16:32], AF.Exp, [DS], [DS])
        tt('dve', DS[:, t, 80:88], LB[:, t, 8:16], CS[:, t, 16:24], ALU.subtract, [LB, CS], [DS])
        for c in range(2):
            po = c * 64
            tt('dve', DS[po:po + 64, t, 64:80], TOT[po:po + 64, t, c, 0:16], CS[po:po + 64, t, 0:16], ALU.subtract,
               [TOT, CS], [DS])
            tt('dve', DS[po:po + 64, t, 96:104], DS[po:po + 64, t, 80:88], TOT[po:po + 64, t, c, 16:24], ALU.add,
               [TOT, DS], [DS])
        act(DS[:, t, 64:80], DS[:, t, 64:80], AF.Exp, [DS], [DS])
        act(DS[:, t, 80:88], DS[:, t, 80:88], AF.Exp, [DS], [DS])
        act(DS[:, t, 96:104], DS[:, t, 96:104], AF.Exp, [DS], [DS])
        act(DS[:, t, 88:96], CS[:, t, 16:24], AF.Exp, [CS], [DS])
    scr = nc.dram_tensor("scr_fm", [32, NT], F32).ap()
    SCRt = fw.view("scrfm", scr)
    fw.dma('sp', scr[0:16, :], FMF[:], r=[FMF], w=[SCRt])
    fw.dma('sp', scr[16:32, :], FMB[:], r=[FMB], w=[SCRt])
    EMF = fw.sbs("emf", [128, 32], F32)
    fw.scope_begin()
    LFT = fw.sbs("lft", [8, NT], F32)
    GMT = fw.sbs("gmt", [8, NT], F32)
    GTMS = fw.sbs("gtms", [128, NTILE, 8], F32)
    GOFF = fw.sbs("goff", [128, 8], F32)
    GTP = fw.sbs("gtp", [128, 8], F32)
    for slot in range(4):
        chunks = [(2 * slot, 0), (2 * slot, 1), (2 * slot + 1, 0), (2 * slot + 1, 1)]
        for d in range(2):
            order = chunks if d == 0 else chunks[::-1]
            cs0, lb0, tc0 = 16 + 4 * d, 8 + 4 * d, 16 + 4 * d
            oc = slice(4 * d, 4 * d + 4)
            for qi, (t, c) in enumerate(order):
                po = c * 64
                if qi == 0:
                    tt('dve', GTMS[po:po + 64, t, oc], LB[po:po + 64, t, lb0:lb0 + 4], CS[po:po + 64, t, cs0:cs0 + 4],
                       ALU.subtract, [LB, CS], [GTMS])
                else:
                    tp, cprev = order[qi - 1]
                    if qi == 1:
                        cp('pool', GOFF[:, oc], TOT[:, tp, cprev, tc0:tc0 + 4], [TOT], [GOFF])
                    else:
                        tt('pool', GOFF[:, oc], GOFF[:, oc], TOT[:, tp, cprev, tc0:tc0 + 4], ALU.add, [GOFF, TOT], [GOFF])
                    tt('dve', GTP[po:po + 64, oc], CS[po:po + 64, t, cs0:cs0 + 4], GOFF[po:po + 64, oc], ALU.add,
                       [CS, GOFF], [GTP])
                    tt('dve', GTMS[po:po + 64, t, oc], LB[po:po + 64, t, lb0:lb0 + 4], GTP[po:po + 64, oc],
                       ALU.subtract, [LB, GTP], [GTMS])
    for t in range(NTILE):
        ps = PSM.get()
        mm(ps[0:8, :], GTMS[:, t, :], cf('ident'), True, True, [GTMS, CF], [ps])
        cp('dve', GMT[0:8, t * 128:(t + 1) * 128], ps[0:8, :], [ps], [GMT])
        ps = PSM.get()
        mm(ps[0:8, :], LB[:, t, 0:8], cf('ident'), True, True, [LB, CF], [ps])
        cp('act', LFT[0:8, t * 128:(t + 1) * 128], ps[0:8, :], [ps], [LFT])
    MF = fw.sbs("mf", [8, 4], F32)
    MFx = fw.sbs("mfx", [8, 4], F32)
    fw.op('dve', lambda e: e.tensor_reduce(MFx[:], GMT[:].rearrange("r (s t) -> r s t", s=4), AX.X, ALU.max),
          r=[GMT], w=[MFx])
    fw.op('dve', lambda e: e.tensor_reduce(MF[:], LFT[:].rearrange("r (s t) -> r s t", s=4), AX.X, ALU.add),
          r=[LFT], w=[MF])
    ts('dve', MFx[:], MFx[:], 0.0, None, ALU.max, None, [MFx], [MFx])
    tt('dve', MF[:], MF[:], MFx[:], ALU.add, [MF, MFx], [MF])
    fw.dma('sp', st_m.rearrange("s r -> r s"), MF[:], r=[MF], allow_slow_non_contiguous=True)
    XD = fw.sbs("xd", [8, 8, 4], F32)
    EMFs = fw.sbs("emfs", [8, 4], F32)
    act(EMFs[:], MF[:], AF.Exp, [MF], [EMFs], scale=-1.0)
    for sl in range(4):
        ts('dve', XD[:, :, sl], cf('ident')[0:8, 0:8], EMFs[:, sl:sl + 1], None, ALU.mult, None, [CF, EMFs], [XD])
    ONES8 = fw.sbs("ones8", [8, 128], F32)
    memset('pool', ONES8[:], 1.0, [ONES8])
    ps = PSM.get()
    mm(ps[:, 0:32], ONES8[:], XD[:].rearrange("r a s -> r (a s)"), True, True, [ONES8, XD], [ps])
    cp('dve', EMF[:], ps[:, 0:32], [ps], [EMF])
    fw.scope_end()

    def proj128(col0, evac):
        wb, wv = load_w(w_in[:, col0:col0 + 128], 128)
        for tb in range(2):
            pb = PBIG.get()
            for kc in range(KC):
                mm(pb[:], wv[:, kc, :], hT[:, kc, tb * 512:(tb + 1) * 512], kc == 0, kc == KC - 1, [wb, hT], [pb])
            evac(pb, tb)

    def slot_of(t):
        return t // 2

    def run_rr(gens):
        gens = list(gens)
        while gens:
            for g in list(gens):
                try:
                    next(g)
                except StopIteration:
                    gens.remove(g)

    fw.scope_begin()
    RAW = fw.sbs("raw", [128, NT], F32)
    GRAWs = [fw.sbs("graw%d" % i, [128, NT], BF16) for i in range(2)]
    Y3 = [fw.sbs("ycv%d" % i, [128, NT], F32) for i in range(3)]
    SQ = fw.sbs("sqb", [128, NT], BF16)
    RVT = fw.sbs("rvt", [128, NT], F32)
    QNs = [fw.sbs("qn%d" % i, [128, NT], BF16) for i in range(2)]
    KNBs = [fw.sbs("knb%d" % i, [128, NT], BF16) for i in range(2)]
    KTs = [fw.sbs("kt%d" % i, [128, NTILE, 128], BF16) for i in range(2)]
    VTs = [fw.sbs("vt%d" % i, [128, NTILE, 128], BF16) for i in range(2)]
    OO = [fw.sbs("oo%d" % i, [128, NTILE, 128], F32) for i in range(2)]
    OSS = fw.sbs("oss", [128, NTILE], F32)
    JNK = fw.sbs("jnk", [128, 128], BF16)
    UG = [RP(fw, "ug%d_" % i, [128, 128], F32, 4, scoped=True) for i in range(2)]
    BC = [RP(fw, "bc%d_" % i, [128, 2, 128], F32, 2, scoped=True) for i in range(2)]
    CH = [[RP(fw, "ch%d_%d_" % (i, j), [128, 128], F32, 8, scoped=True) for j in range(2)] for i in range(2)]
    KK = [RP(fw, "kk%d_" % i, [128, 128], F32, 1, scoped=True) for i in range(2)]
    QK = [RP(fw, "qk%d_" % i, [128, 128], F32, 1, scoped=True) for i in range(2)]
    UU = [RP(fw, "uu%d_" % i, [128, 128], F32, 3, scoped=True) for i in range(2)]
    TTB = [RP(fw, "ttb%d_" % i, [128, 128], BF16, 2, scoped=True) for i in range(2)]
    WT = [RP(fw, "wt%d_" % i, [128, 128], BF16, 3, scoped=True) for i in range(2)]
    ATT = [RP(fw, "att%d_" % i, [128, 128], BF16, 3, scoped=True) for i in range(2)]
    QG = [RP(fw, "qg%d_" % i, [128, 128], BF16, 3, scoped=True) for i in range(2)]
    KBG = [RP(fw, "kbg%d_" % i, [128, 128], BF16, 2, scoped=True) for i in range(2)]
    KD = [RP(fw, "kd%d_" % i, [128, 128], BF16, 3, scoped=True) for i in range(2)]
    VBt = [RP(fw, "vbt%d_" % i, [128, 128], BF16, 2, scoped=True) for i in range(2)]
    VN = [RP(fw, "vn%d_" % i, [128, 128], BF16, 2, scoped=True) for i in range(2)]
    SA = [fw.sbs("sa%d" % i, [128, 128], F32) for i in range(2)]
    SAb = [fw.sbs("sab%d" % i, [128, 128], BF16) for i in range(2)]

    PRE_PS = [[[PSM.t[(dd * 2 + par) * 3 + j] for j in range(3)] for par in range(2)] for dd in range(2)]
    SCAN_PS = [[SubTile(PBIG.t[dd], PBIG.t[dd][:, j * 128:(j + 1) * 128], "pscan%d_%d" % (dd, j)) for j in range(3)]
               for dd in range(2)]
    PSS = RP.__new__(RP)
    PSS.t = PSM.t[12:15]
    PSS.i = 0

    def conv_silu_gen(dst, ci):
        x3 = RAW[:].rearrange("p (s t) -> p s t", s=4)
        y3 = dst[:].rearrange("p (s t) -> p s t", s=4)
        x4 = RAW[:].rearrange("p (s c t) -> p s c t", s=4, c=4)
        y4 = dst[:].rearrange("p (s c t) -> p s c t", s=4, c=4)
        act(dst[:], RAW[:], AF.Copy, [RAW, CW], [dst], scale=CW[:, 1, ci:ci + 1])
        yield
        stt('dve', y3[:, :, 1:256], x3[:, :, 0:255], CW[:, 0, ci:ci + 1], y3[:, :, 1:256], ALU.mult, ALU.add,
            [RAW, CW, dst], [dst])
        yield
        stt('dve', y3[:, :, 0:255], x3[:, :, 1:256], CW[:, 2, ci:ci + 1], y3[:, :, 0:255], ALU.mult, ALU.add,
            [RAW, CW, dst], [dst])
        yield
        stt('dve', y4[:, :, 1:4, 0], x4[:, :, 0:3, 63], CWF[:, 0, ci:ci + 1], y4[:, :, 1:4, 0], ALU.mult, ALU.add,
            [RAW, CWF, dst], [dst])
        stt('dve', y4[:, :, 0:3, 63], x4[:, :, 1:4, 0], CWF[:, 1, ci:ci + 1], y4[:, :, 0:3, 63], ALU.mult, ALU.add,
            [RAW, CWF, dst], [dst])
        yield
        act(dst[:], dst[:], AF.Silu, [dst], [dst])
        yield

    def rinv_gen(src):
        tt('pool', SQ[:], src[:], src[:], ALU.mult, [src], [SQ])
        yield
        for tb in range(2):
            mm(PBIG.t[2][:], ONESB[:], SQ[:, tb * 512:(tb + 1) * 512], True, True, [ONESB, SQ], [PBIG.t[2]])
            yield
            act(RVT[:, tb * 512:(tb + 1) * 512], PBIG.t[2][:], AF.Ln, [PBIG.t[2], EPSC], [RVT], bias=EPSC[:, 0:1])
            yield
        act(RVT[:], RVT[:], AF.Exp, [RVT], [RVT], scale=-0.5)
        yield

    def proj_gen(col0, evac):
        wb, wv = load_w(w_in[:, col0:col0 + 128], 128)
        for tb in range(2):
            pb = PBIG.t[2] if tb == 0 else PN
            for kc in range(KC):
                mm(pb[:], wv[:, kc, :], hT[:, kc, tb * 512:(tb + 1) * 512], kc == 0, kc == KC - 1, [wb, hT], [pb])
                if kc % 4 == 3:
                    yield
            evac(pb, tb)
            yield

    def mod_gen(nb):
        wb, wv = load_w(w_ada[:, nb * 256:(nb + 1) * 256], 256)
        pb = PBIG.t[2]
        for kc in range(KC):
            mm(pb[0:1, 0:256], sT[:, kc:kc + 1], wv[:, kc, :], kc == 0, kc == KC - 1, [sT, wb], [pb])
            if kc % 4 == 3:
                yield
        rt = ROWT[rowi[0] % 2]
        rowi[0] += 1
        cp('act', rt[:], pb[0:1, 0:256], [pb], [rt])
        yield
        for j in range(2):
            c = nb * 2 + j
            mm(MODPS[:, c:c + 1], rt[0:1, j * 128:(j + 1) * 128], ONE11[0:1, 0:1], True, True, [rt, ONE11], [MODPS])
        yield

    def a_prologue(h):
        b = h % 2
        QN, KNB, KT, VT, GRAW = QNs[b], KNBs[b], KTs[b], VTs[b], GRAWs[b]

        def ev_raw(pb, tb):
            cp('act' if tb == 0 else 'dve', RAW[:, tb * 512:(tb + 1) * 512], pb[:], [pb], [RAW])

        def ev_gate(pb, tb):
            act(GRAW[:, tb * 512:(tb + 1) * 512], pb[:], AF.Silu, [pb], [GRAW])

        for i, base in enumerate((0, 1024, 2048)):
            yield from proj_gen(base + h * 128, ev_raw)
            yield from conv_silu_gen(Y3[i], i * 8 + h)
        yield from proj_gen(3072 + h * 128, ev_gate)
        if h >= 1:
            for nb in range(16 + 4 * (h - 1), 20 + 4 * (h - 1)):
                yield from mod_gen(nb)
        yield from rinv_gen(Y3[0])
        stt('dve', QN[:], Y3[0][:], float(128 ** -0.5), RVT[:], ALU.mult, ALU.mult, [Y3[0], RVT], [QN])
        yield
        yield from rinv_gen(Y3[1])
        tt('dve', Y3[1][:], Y3[1][:], RVT[:], ALU.mult, [Y3[1], RVT], [Y3[1]])
        yield
        cp('pool', KNB[:], Y3[1][:], [Y3[1]], [KNB])
        yield
        for t in range(NTILE):
            ps = PSS.get()
            tr(ps[:], Y3[1][:, t * 128:(t + 1) * 128], cf('ident'), [Y3[1], CF], [ps])
            cp('act', KT[:, t, :], ps[:], [ps], [KT])
            yield
            ps = PSS.get()
            tr(ps[:], Y3[2][:, t * 128:(t + 1) * 128], cf('ident'), [Y3[2], CF], [ps])
            cp('dve', VT[:, t, :], ps[:], [ps], [VT])
            yield

    def run_bg(gens, bg):
        gens = list(gens)
        while gens:
            for g in list(gens):
                try:
                    next(g)
                except StopIteration:
                    gens.remove(g)
            if bg[0] is not None:
                try:
                    next(bg[0])
                except StopIteration:
                    bg[0] = None

    pre_out = {}
    pre_g = {}

    def a_pre(h, s, d):
        QN, KNB, KT, VT = QNs[h % 2], KNBs[h % 2], KTs[h % 2], VTs[h % 2]
        t = s if d == 0 else NTILE - 1 - s
        col = d * 8 + h
        tok = slice(t * 128, (t + 1) * 128)
        CHp = CH[d][s % 2]
        DPS = PRE_PS[d][s % 2]
        bc = BC[d].get()
        g0 = fw.dma('sp', bc[:, 0, :], scr[16 * d + 8 + h:16 * d + 9 + h, tok].partition_broadcast(128),
                    r=[SCRt], w=[bc])
        fw.dma('sp', bc[:, 1, :], scr[16 * d + h:16 * d + h + 1, tok].partition_broadcast(128),
               r=[SCRt], w=[bc], group=g0)
        ps = PSS.get()
        mm(ps[:], KNB[:, tok], KNB[:, tok], True, True, [KNB], [ps])
        kk = KK[d].get()
        cp('act', kk[:], ps[:], [ps], [kk])
        ps = PSS.get()
        mm(ps[:], KNB[:, tok], QN[:, tok], True, True, [KNB, QN], [ps])
        qk = QK[d].get()
        cp('dve', qk[:], ps[:], [ps], [qk])
        yield
        mo, _ = offs['maskf' if d == 0 else 'maskb']
        gM, gL, e1, egt = UG[d].get(), UG[d].get(), UG[d].get(), UG[d].get()
        tt('dve', gM[:], bc[:, 0, :], CF[:, mo:mo + 128], ALU.add, [bc, CF], [gM])
        tt('dve', gL[:], bc[:, 1, :], CF[:, mo + 128:mo + 256], ALU.add, [bc, CF], [gL])
        tt('dve', e1[:], bc[:, 1, :], CF[:, mo + 256:mo + 384], ALU.add, [bc, CF], [e1])
        yield
        act(gM[:], gM[:], AF.Exp, [gM, DS], [gM], bias=DS[:, t, col:col + 1])
        act(gL[:], gL[:], AF.Exp, [gL, DS], [gL], bias=DS[:, t, 16 + col:17 + col], scale=-1.0)
        act(e1[:], e1[:], AF.Exp, [e1, DS], [e1], bias=DS[:, t, col:col + 1])
        act(egt[:], bc[:, 1, :], AF.Exp, [bc], [egt])
        kbg, kd, vb = KBG[d].get(), KD[d].get(), VBt[d].get()
        act(kbg[:], KT[:, t, :], AF.Copy, [KT, DS], [kbg], scale=DS[:, t, 48 + col:49 + col])
        ts('dve', kd[:], KT[:, t, :], DS[:, t, 64 + col:65 + col], None, ALU.mult, None, [KT, DS], [kd])
        act(vb[:], VT[:, t, :], AF.Copy, [VT, DS], [vb], scale=DS[:, t, 32 + col:33 + col])
        yield
        M, L = CHp.get(), CHp.get()
        tt('pool', M[:].bitcast(F32R), kk[:], gM[:], ALU.mult, [kk, gM], [M])
        tt('pool', L[:].bitcast(F32R), kk[:], gL[:], ALU.mult, [kk, gL], [L])
        att = ATT[d].get()
        tt('dve', att[:], qk[:], e1[:], ALU.mult, [qk, e1], [att])
        qg = QG[d].get()
        tt('pool', qg[:], QN[:, tok], egt[:], ALU.mult, [QN, egt], [qg])
        R = CHp.get()
        tt('pool', R[:].bitcast(F32R), cf('ident'), M[:], ALU.subtract, [CF, M], [R])
        yield
        P, Q = M, L
        rps = None
        Qprev = None
        for k in range(1, 7):
            if k <= 5:
                qps = DPS[0]
                mm(qps[:], P[:].bitcast(F32R), Q[:].bitcast(F32R), True, True, [P, Q], [qps])
                if k < 5:
                    pps = DPS[1]
                    mm(pps[:], Q[:].bitcast(F32R), P[:].bitcast(F32R), True, True, [P, Q], [pps])
            if k >= 2:
                rps = DPS[2]
                mm(rps[:], Q[:].bitcast(F32R), R[:].bitcast(F32R), True, True, [Q, R], [rps])
            yield
            if k == 6:
                ttb = TTB[d].get()
                tt('dve', ttb[:], R[:], rps[:], ALU.add, [R, rps], [ttb])
            elif k >= 2:
                Rn = CHp.get()
                tt('dve', Rn[:].bitcast(F32R), R[:], rps[:], ALU.add, [R, rps], [Rn])
                R = Rn
            if k <= 5:
                Qn = CHp.get()
                cp('act', Qn[:].bitcast(F32R), qps[:], [qps], [Qn])
                if k < 5:
                    Pn = CHp.get()
                    cp('dve', Pn[:].bitcast(F32R), pps[:], [pps], [Pn])
                else:
                    Pn = None
                P, Q = Pn, Qn
            yield
        ups = DPS[0]
        mm(ups[:], ttb[:], vb[:], True, True, [ttb, vb], [ups])
        wps = DPS[1]
        mm(wps[:], kbg[:], ttb[:], True, True, [kbg, ttb], [wps])
        yield
        uu = UU[d].get()
        cp('act', uu[:], ups[:], [ups], [uu])
        wt = WT[d].get()
        cp('dve', wt[:], wps[:], [wps], [wt])
        pre_out[(h, s, d)] = (uu, wt, att, qg, kd)


    bg = [a_prologue(0)]
    run_bg([], bg)
    while bg[0] is not None:
        run_rr([bg[0]])
        bg[0] = None
    for h in range(8):
        QN, KNB, KT, VT, GRAW = QNs[h % 2], KNBs[h % 2], KTs[h % 2], VTs[h % 2], GRAWs[h % 2]
        bg = [a_prologue(h + 1) if h + 1 < 8 else None]
        def a_scan(s, d):
            t = s if d == 0 else NTILE - 1 - s
            col = d * 8 + h
            uu, wt, att, qg, kd = pre_out.pop((h, s, d))
            S, Sb = SA[d], SAb[d]
            for ci in range(2):
                c = ci if d == 0 else 1 - ci
                po = c * 64
                first_in_slot = (t % 2 == 0 and c == 0) if d == 0 else (t % 2 == 1 and c == 1)
                last_in_slot = (t % 2 == 1 and c == 1) if d == 0 else (t % 2 == 0 and c == 0)
                if first_in_slot:
                    if s == 0:
                        fw.dma('sp', S[:], sdelta[d, h], w=[S])
                    else:
                        ts('dve', S[:], S[:], FLG[:, 0:1], None, ALU.mult, None, [S, FLG], [S])
                    cp('act', Sb[:], S[:], [S], [Sb])
                    yield
                wsp = SCAN_PS[d][0]
                mm(wsp[:], wt[:], Sb[:], True, True, [wt, Sb], [wsp])
                yield
                vn = VN[d].get()
                tt('dve', vn[po:po + 64, :], uu[po:po + 64, :], wsp[po:po + 64, :], ALU.subtract, [uu, wsp], [vn])
                yield
                ops_ = SCAN_PS[d][1]
                mm(ops_[:], qg[:], Sb[:], True, False, [qg, Sb], [ops_])
                mm(ops_[:], att[po:po + 64, :], vn[po:po + 64, :], False, True, [att, vn], [ops_])
                kvp = SCAN_PS[d][2]
                mm(kvp[:], kd[po:po + 64, :], vn[po:po + 64, :], True, True, [kd, vn], [kvp])
                yield
                cp('act', OO[d][po:po + 64, t, :], ops_[po:po + 64, :], [ops_], [OO[d]])
                stt('dve', Sb[:], S[:], EG[:, t, c, col:col + 1], kvp[:], ALU.mult, ALU.add, [S, EG, kvp], [Sb])
                stt('dve', S[:], S[:], EG[:, t, c, col:col + 1], kvp[:], ALU.mult, ALU.add, [S, EG, kvp], [S])
                if last_in_slot:
                    sg = STG.get()
                    cp('pool', sg[:, 0:128], S[:], [S], [sg])
                    fw.dma('sp', st_delta[slot_of(t), d, h], sg[:, 0:128], r=[sg])
                yield

        def start_pre(s):
            if s < NTILE and (h, s) not in pre_g:
                pre_g[(h, s)] = [a_pre(h, s, 0), a_pre(h, s, 1)]

        def run_multi(fg, bgs):
            fg = list(fg)
            while fg:
                for g in list(fg):
                    try:
                        next(g)
                    except StopIteration:
                        fg.remove(g)
                for g in list(bgs):
                    try:
                        next(g)
                    except StopIteration:
                        bgs.remove(g)
                for _ in range(2):
                    if bg[0] is not None:
                        try:
                            next(bg[0])
                        except StopIteration:
                            bg[0] = None

        if (h, 0) not in pre_g:
            start_pre(0)
            for _ in range(8):
                for g in pre_g[(h, 0)]:
                    next(g)
        start_pre(1)
        run_multi(pre_g[(h, 0)], pre_g[(h, 1)])
        def a_epi(t):
            tk_ = slice(t * 128, (t + 1) * 128)
            tt('pool', OO[0][:, t, :], OO[0][:, t, :], OO[1][:, t, :], ALU.add, [OO[0], OO[1]], [OO[0]])
            yield
            act(JNK[:], OO[0][:, t, :], AF.Square, [OO[0]], [JNK, OSS], accum=OSS[:, t:t + 1])
            yield
            act(OSS[:, t:t + 1], OSS[:, t:t + 1], AF.Ln, [OSS, EPSC], [OSS], scale=1.0 / 128, bias=EPSC[:, 0:1])
            act(OSS[:, t:t + 1], OSS[:, t:t + 1], AF.Exp, [OSS], [OSS], scale=-0.5)
            yield
            ts('dve', OO[0][:, t, :], OO[0][:, t, :], OSS[:, t:t + 1], None, ALU.mult, None, [OO[0], OSS], [OO[0]])
            yield
            ps = PSS.get()
            tr(ps[:], OO[0][:, t, :], cf('ident'), [OO[0], CF], [ps])
            stt('dve', yT[:, h, tk_], ps[:], NAB[:, 0:1], GRAW[:, tk_], ALU.mult, ALU.mult, [ps, NAB, GRAW], [yT])
            yield

        epis = []
        nxt = []
        for s in range(NTILE):
            start_pre(s + 2)
            if s >= 6 and h + 1 < 8:
                if bg[0] is not None:
                    run_rr([bg[0]])
                    bg[0] = None
                pre_g[(h + 1, s - 6)] = [a_pre(h + 1, s - 6, 0), a_pre(h + 1, s - 6, 1)]
                nxt = nxt + pre_g[(h + 1, s - 6)]
            run_multi([a_scan(s, 0), a_scan(s, 1)] + pre_g.get((h, s + 1), []), pre_g.get((h, s + 2), []) + epis + nxt)
            if s >= 4:
                epis = epis + [a_epi(s), a_epi(NTILE - 1 - s)]
        if bg[0] is not None:
            run_rr([bg[0]])
        run_rr(epis)
        if h == 7:
            for nb in range(44, 48):
                mod_block(nb)
            mod_finish()
        if stage == 2:
            def dump(name, ap, shape):
                o = dout("dbg_" + name, shape)
                fw.dma('sp', o, ap, r=list(fw.tiles))
            dump("GX", GX[:].rearrange("p a b -> p (a b)"), [128, NTILE * 64])
            dump("LB", LB[:].rearrange("p a b -> p (a b)"), [128, NTILE * 16])
            dump("CS", CS[:].rearrange("p a b -> p (a b)"), [128, NTILE * 24])
            dump("DS", DS[:].rearrange("p a b -> p (a b)"), [128, NTILE * 104])
            dump("TOT", TOT[:].rearrange("p a b c -> p (a b c)"), [128, NTILE * 48])
            dump("FMF", FMF[:], [16, NT])
            dump("KN", Y3[1][:], [128, NT])
            dump("V", Y3[2][:], [128, NT])
            dump("Q", Y3[0][:], [128, NT])
            dump("RVT", RVT[:], [128, NT])
            dump("OOf", OO[0][:].rearrange("p a b -> p (a b)"), [128, NTILE * 128])
            dump("OOb", OO[1][:].rearrange("p a b -> p (a b)"), [128, NTILE * 128])
            fw.emit()
            return nc, fw
    fw.scope_end()
    fw.scope_begin()
    QB = fw.sbs("qb", [128, NT], BF16)
    KBF = fw.sbs("kbf", [128, NT], F32)
    KBB = fw.sbs("kbb", [128, NT], BF16)
    VBR = fw.sbs("vbr", [128, 2, NT], F32)
    OG = fw.sbs("og", [128, 2, NT], F32)
    KTB = fw.sbs("ktb", [128, NTILE, 128], F32)
    VE = fw.sbs("ve", [128, NTILE, 264], BF16)
    memset('pool', VE[:, :, 256:257], 1.0, [VE])
    HB = [fw.sbs("hb%d" % i, [128, NTILE, 256], F32) for i in range(2)]
    HSS = fw.sbs("hss", [128, NTILE], F32)
    JNK2 = fw.sbs("jnk2", [128, 256], BF16)
    KQ = RP(fw, "kq", [128, 128], F32, 2, scoped=True)
    DW = [RP(fw, "dw%d_" % i, [128, 128], BF16, 2, scoped=True) for i in range(2)]
    KP = [RP(fw, "kp%d_" % i, [128, 128], BF16, 2, scoped=True) for i in range(2)]
    TMB = [RP(fw, "tmb%d_" % i, [128, 2], F32, 2, scoped=True) for i in range(2)]
    CN = [fw.sbs("cn%d" % i, [128, 264], F32) for i in range(2)]
    CNb = [fw.sbs("cnb%d" % i, [128, 264], BF16) for i in range(2)]

    def evac_q(pb, tb):
        act(QB[:, tb * 512:(tb + 1) * 512], pb[:], AF.Copy, [pb], [QB], scale=float(128 ** -0.5))

    def evac_k(pb, tb):
        cp('dve', KBF[:, tb * 512:(tb + 1) * 512], pb[:], [pb], [KBF])
        cp('act', KBB[:, tb * 512:(tb + 1) * 512], pb[:], [pb], [KBB])

    def evac_v(j):
        def f(pb, tb):
            cp('dve' if tb else 'act', VBR[:, j, tb * 512:(tb + 1) * 512], pb[:], [pb], [VBR])
        return f

    def evac_o(j):
        def f(pb, tb):
            act(OG[:, j, tb * 512:(tb + 1) * 512], pb[:], AF.Sigmoid, [pb], [OG])
        return f

    for hb in range(4):
        proj128(4128 + hb * 128, evac_q)
        proj128(4640 + hb * 128, evac_k)
        for j in range(2):
            proj128(5152 + hb * 256 + j * 128, evac_v(j))
            proj128(6176 + hb * 256 + j * 128, evac_o(j))
        for t in range(NTILE):
            tok = slice(t * 128, (t + 1) * 128)
            ps = PSM.get()
            tr(ps[:], KBF[:, tok], cf('ident'), [KBF, CF], [ps])
            cp('act', KTB[:, t, :], ps[:], [ps], [KTB])
            for j in range(2):
                ps = PSM.get()
                tr(ps[:], VBR[:, j, tok], cf('ident'), [VBR, CF], [ps])
                cp('dve', VE[:, t, j * 128:(j + 1) * 128], ps[:], [ps], [VE])
        def b_pre(s, d):
            t = s if d == 0 else NTILE - 1 - s
            col = d * 4 + hb
            tok = slice(t * 128, (t + 1) * 128)
            ps = PSM.get()
            mm(ps[:], KBB[:, tok], QB[:, tok], True, True, [KBB, QB], [ps])
            yield
            dw = DW[d].get()
            stt('dve', dw[:], ps[:], DS[:, t, 80 + col:81 + col], cf('m01f' if d == 0 else 'm01b'),
                ALU.mult, ALU.mult, [ps, DS, CF], [dw])
            kp = KP[d].get()
            act(kp[:], KTB[:, t, :], AF.Copy, [KTB, DS], [kp], scale=DS[:, t, 96 + col:97 + col])
            bpre_out[(s, d)] = (dw, kp)

        def b_scan(s, d):
            t = s if d == 0 else NTILE - 1 - s
            col = d * 4 + hb
            tok = slice(t * 128, (t + 1) * 128)
            dw, kp = bpre_out.pop((s, d))
            C, Cb = CN[d], CNb[d]
            for ci in range(2):
                c = ci if d == 0 else 1 - ci
                po = c * 64
                first_in_slot = (t % 2 == 0 and c == 0) if d == 0 else (t % 2 == 1 and c == 1)
                last_in_slot = (t % 2 == 1 and c == 1) if d == 0 else (t % 2 == 0 and c == 0)
                if first_in_slot:
                    if s == 0:
                        fw.dma('sp', C[:, 0:256], sC[d, hb], w=[C])
                        fw.dma('sp', C[:, 256:257], sn[d, hb, :].rearrange("(p o) -> p o", o=1), w=[C])
                        ts('dve', C[:, 0:257], C[:, 0:257], EM0[:, col:col + 1], None, ALU.mult, None,
                           [C, EM0], [C])
                    else:
                        ts('dve', C[:, 0:257], C[:, 0:257], FLG[:, 0:1], None, ALU.mult, None, [C, FLG], [C])
                    cp('act', Cb[:, 0:257], C[:, 0:257], [C], [Cb])
                    yield
                nd = PBIG.t[d]
                mm(nd[:, 0:257], QB[:, tok], Cb[:, 0:257], True, False, [QB, Cb], [nd])
                mm(nd[:, 0:257], dw[po:po + 64, :], VE[po:po + 64, t, 0:257], False, True, [dw, VE], [nd])
                dc = PBIG.t[2] if d == 0 else PN
                mm(dc[:, 0:257], kp[po:po + 64, :], VE[po:po + 64, t, 0:257], True, True, [kp, VE], [dc])
                yield
                tm = TMB[d].get()
                act(tm[po:po + 64, 0:1], nd[po:po + 64, 256:257], AF.Abs, [nd, DS], [tm],
                    scale=DS[po:po + 64, t, 88 + col:89 + col])
                stt('dve', Cb[:, 0:257], C[:, 0:257], EG[:, t, c, 16 + col:17 + col], dc[:, 0:257],
                    ALU.mult, ALU.add, [C, EG, dc], [Cb])
                stt('dve', C[:, 0:257], C[:, 0:257], EG[:, t, c, 16 + col:17 + col], dc[:, 0:257],
                    ALU.mult, ALU.add, [C, EG, dc], [C])
                ts('dve', tm[po:po + 64, 0:1], tm[po:po + 64, 0:1], 1.0, None, ALU.max, None, [tm], [tm])
                recip(tm[po:po + 64, 0:1], tm[po:po + 64, 0:1], [tm], [tm])
                tt('dve', tm[po:po + 64, 1:2], tm[po:po + 64, 0:1], DS[po:po + 64, t, 88 + col:89 + col], ALU.mult,
                   [tm, DS], [tm])
                yield
                act(HB[d][po:po + 64, t, :], nd[po:po + 64, 0:256], AF.Copy, [nd, tm], [HB[d]],
                    scale=tm[po:po + 64, 1:2])
                if last_in_slot:
                    sl = slot_of(t)
                    sg = STG.get()
                    ts('dve', sg[:, 0:257], C[:, 0:257], EMF[:, col * 4 + sl:col * 4 + sl + 1], None, ALU.mult,
                       None, [C, EMF], [sg])
                    fw.dma('sp', st_C[sl, d, hb], sg[:, 0:256], r=[sg])
                    fw.dma('sp', st_n[sl, d, hb, :].rearrange("(p o) -> p o", o=1), sg[:, 256:257], r=[sg])
                yield

        bpre_out = {}
        run_rr([b_pre(0, 0), b_pre(0, 1)])
        for s in range(NTILE):
            gens = [b_scan(s, 0), b_scan(s, 1)]
            if s + 1 < NTILE:
                gens += [b_pre(s + 1, 0), b_pre(s + 1, 1)]
            run_rr(gens)
        for t in range(NTILE):
            tt('pool', HB[0][:, t, :], HB[0][:, t, :], HB[1][:, t, :], ALU.add, [HB[0], HB[1]], [HB[0]])
            act(JNK2[:], HB[0][:, t, :], AF.Square, [HB[0]], [JNK2, HSS], accum=HSS[:, t:t + 1])
        act(HSS[:], HSS[:], AF.Ln, [HSS, EPSC], [HSS], scale=1.0 / 256, bias=EPSC[:, 0:1])
        act(HSS[:], HSS[:], AF.Exp, [HSS], [HSS], scale=-0.5)
        for t in range(NTILE):
            ts('dve', HB[0][:, t, :], HB[0][:, t, :], HSS[:, t:t + 1], None, ALU.mult, None, [HB[0], HSS], [HB[0]])
            for j in range(2):
                ps = PSM.get()
                tr(ps[:], HB[0][:, t, j * 128:(j + 1) * 128], cf('ident'), [HB[0], CF], [ps])
                stt('dve', yT[:, 8 + 2 * hb + j, t * 128:(t + 1) * 128], ps[:], NAB[:, 1 + j:2 + j],
                    OG[:, j, t * 128:(t + 1) * 128], ALU.mult, ALU.mult, [ps, NAB, OG], [yT])
    fw.scope_end()
    fw.scope_end()

    NH = 512
    fw.scope_begin()
    X1 = fw.sbs("x1", [128, KC, NH], F32)
    XY = fw.sbs("xy", [128, D], F32)
    RV = fw.sbs("rv", [128, NH], F32)
    SQT = [fw.sbs("sqt%d" % i, [128, NH], BF16) for i in range(2)]
    RL = [fw.sbs("rl%d" % i, [128, NH], F32) for i in range(2)]
    shift2 = modT[:, 48:64]

    def rms_from_pn(nfeat):
        act(RV[:], PN[:], AF.Ln, [PN, EPSC], [RV], scale=1.0 / nfeat, bias=EPSC[:, 0:1])
        act(RV[:], RV[:], AF.Exp, [RV], [RV], scale=-0.5)

    for hf in range(2):
        tk = slice(hf * NH, (hf + 1) * NH)
        fw.scope_begin()
        MIX = fw.sbs("mix%d" % hf, [128, KC, NH], F32)
        for cbp in range(KC // 2):
            wb, wv = load_w(w_out[:, cbp * 256:(cbp + 1) * 256], 256)
            for j in range(2):
                cbk = cbp * 2 + j
                pb = PBIG.get()
                for kc in range(KC):
                    mm(pb[:], wv[:, kc, j * 128:(j + 1) * 128], yT[:, kc, tk], kc == 0, kc == KC - 1, [wb, yT], [pb])
                cp('dve', MIX[:, cbk, :], pb[:], [pb], [MIX])
                sq = SQT[cbk % 2]
                act(sq[:], pb[:], AF.Square, [pb], [sq])
                mm(PN[:], ONESB[:], sq[:], cbk == 0, cbk == KC - 1, [ONESB, sq], [PN])
        rms_from_pn(D)
        for c in range(KC):
            tt('dve', MIX[:, c, :], MIX[:, c, :], RV[:], ALU.mult, [MIX, RV], [MIX])
        for t4 in range(4):
            fw.dma('sp', XY[:], xin[hf * NH + t4 * 128:hf * NH + (t4 + 1) * 128, :], w=[XY])
            for c in range(KC):
                ps = PSM.get()
                tr(ps[:], XY[:, c * 128:(c + 1) * 128], cf('ident'), [XY, CF], [ps])
                stt('dve', X1[:, c, t4 * 128:(t4 + 1) * 128], MIX[:, c, t4 * 128:(t4 + 1) * 128], g1p[:, c:c + 1],
                    ps[:], ALU.mult, ALU.add, [MIX, g1p, ps], [X1])
        fw.scope_end()
        fw.scope_begin()
        RT = fw.sbs("rt%d" % hf, [128, 64, NH], BF16)
        fw.scope_begin()
        H2 = fw.sbs("h2%d" % hf, [128, KC, NH], BF16)
        for c in range(KC):
            sq = SQT[c % 2]
            act(sq[:], X1[:, c, :], AF.Square, [X1], [sq])
            mm(PN[:], ONESB[:], sq[:], c == 0, c == KC - 1, [ONESB, sq], [PN])
        rms_from_pn(D)
        for c in range(KC):
            rl = RL[c % 2]
            tt('dve', rl[:], X1[:, c, :], RV[:], ALU.mult, [X1, RV], [rl])
            act(H2[:, c, :], rl[:], AF.Identity, [rl, a2, modT], [H2], scale=a2[:, c:c + 1], bias=shift2[:, c:c + 1])
        for fbp in range(32):
            wb, wv = load_w(w1[:, fbp * 256:(fbp + 1) * 256], 256)
            for j in range(2):
                fb = fbp * 2 + j
                pb = PBIG.get()
                for kc in range(KC):
                    mm(pb[:], wv[:, kc, j * 128:(j + 1) * 128], H2[:, kc, :], kc == 0, kc == KC - 1, [wb, H2], [pb])
                rl = RL[fb % 2]
                act(rl[:], pb[:], AF.Relu, [pb], [rl])
                tt('dve', RT[:, fb, :], rl[:], rl[:], ALU.mult, [rl], [RT])
        fw.scope_end()
        FT = fw.sbs("ft%d" % hf, [128, KC, NH], F32)
        for cbp in range(KC // 2):
            pbs = [PBIG.get(), PBIG.get()]
            for kq in range(4):
                wb, wv = load_w(w2[kq * 2048:(kq + 1) * 2048, cbp * 256:(cbp + 1) * 256], 256)
                for kc in range(KC):
                    kg = kq * KC + kc
                    for j in range(2):
                        mm(pbs[j][:], wv[:, kc, j * 128:(j + 1) * 128], RT[:, kg, :], kg == 0, kg == 63,
                           [wb, RT], [pbs[j]])
            for j in range(2):
                cbk = cbp * 2 + j
                cp('dve', FT[:, cbk, :], pbs[j][:], [pbs[j]], [FT])
                sq = SQT[cbk % 2]
                act(sq[:], pbs[j][:], AF.Square, [pbs[j]], [sq])
                mm(PN[:], ONESB[:], sq[:], cbk == 0, cbk == KC - 1, [ONESB, sq], [PN])
        rms_from_pn(D)
        for c in range(KC):
            tt('dve', FT[:, c, :], FT[:, c, :], RV[:], ALU.mult, [FT, RV], [FT])
            stt('dve', FT[:, c, :], FT[:, c, :], g2p[:, c:c + 1], X1[:, c, :], ALU.mult, ALU.add, [FT, g2p, X1], [FT])
        for t4 in range(4):
            for c in range(KC):
                ps = PSM.get()
                tr(ps[:], FT[:, c, t4 * 128:(t4 + 1) * 128], cf('ident'), [FT, CF], [ps])
                cp('act' if c % 2 else 'dve', XY[:, c * 128:(c + 1) * 128], ps[:], [ps], [XY])
            fw.dma('sp', y[hf * NH + t4 * 128:hf * NH + (t4 + 1) * 128, :], XY[:], r=[XY])
        fw.scope_end()
    fw.scope_end()
    fw.emit()
    return nc, fw


PROMPT_SEQS = {2: [0, 1, 2], 3: [3, 4, 5], 4: [6, 7, 8], 5: [9, 10, 11], 6: [12, 13], 7: [14, 15]}


def make_in_maps(inp):
    cat, offs, sel = make_consts()
    f = np.float32
    shared = {
        'w_ada': np.ascontiguousarray(inp['w_ada'][0]),
        'b_ada': np.ascontiguousarray(inp['b_ada'][0][None, :]),
        'nrm': np.ascontiguousarray(np.stack([inp['norm_mix_pre'][0], inp['norm_mix_post'][0],
                                              inp['norm_ffn_pre'][0], inp['norm_ffn_post'][0]], axis=0)),
        'w_in': np.ascontiguousarray(inp['w_in'][0]),
        'conv_w': np.ascontiguousarray(inp['conv_w'][0]),
        'gparams': np.ascontiguousarray(np.concatenate([inp['a_log'][0].reshape(16), inp['dt_bias'][0].reshape(16),
                                                        inp['mlstm_ibias'][0].reshape(8),
                                                        inp['mlstm_fbias'][0].reshape(8)])[None, :]),
        'norm_a': np.ascontiguousarray(inp['norm_a'].reshape(1, 128)),
        'norm_b': np.ascontiguousarray(inp['norm_b'].reshape(1, 256)),
        'w_out': np.ascontiguousarray(inp['w_out'][0]),
        'w1': np.ascontiguousarray(inp['w_ffn1'][0]),
        'w2': np.ascontiguousarray(inp['w_ffn2'][0]),
        'cst': cat,
    }
    maps = []
    for core in range(NCORES):
        m = dict(shared)
        if core < 2:
            b = core
            m['xin'] = np.ascontiguousarray(inp['x_sample'][b])
            m['cond'] = np.ascontiguousarray(inp['c'][b][None, :])
            m['flags'] = np.ones((128, 2), f)
            m['sdelta'] = np.ascontiguousarray(inp['state_delta'][b, 0])
            m['sC'] = np.ascontiguousarray(inp['state_mlstm_C'][b, 0])
            m['sn'] = np.ascontiguousarray(inp['state_mlstm_n'][b, 0])
            m['sm'] = np.ascontiguousarray(inp['state_mlstm_m'][b, 0].reshape(1, 8))
        else:
            xs = np.zeros((NT, D), f)
            seqs = PROMPT_SEQS[core]
            for s in range(NSLOT):
                xs[s * 256:(s + 1) * 256] = inp['x_prompt'][seqs[s] if s < len(seqs) else seqs[0]]
            m['xin'] = xs
            m['cond'] = np.ascontiguousarray(inp['c_ctx'][None, :])
            m['flags'] = np.zeros((128, 2), f)
            m['sdelta'] = np.zeros((2, 8, 128, 128), f)
            m['sC'] = np.zeros((2, 4, 128, 256), f)
            m['sn'] = np.zeros((2, 4, 128), f)
            m['sm'] = np.zeros((1, 8), f)
        maps.append(m)
    return maps


_CACHE = {}


def kernel(**inputs):
    inp = {k: np.asarray(v) for k, v in inputs.items()}
    maps = make_in_maps(inp)
    if 'nc' not in _CACHE:
        _CACHE['nc'] = build()[0]
    nc = _CACHE['nc']
    res = run_bass_kernel_spmd(nc, maps, core_ids=list(range(NCORES)))
    r = res.results
    f = np.float32
    y_prompt = np.zeros((16, 256, D), f)
    y_sample = np.zeros((2, NT, D), f)
    sd = np.zeros((16, 1, 2, 8, 128, 128), f)
    sCo = np.zeros((16, 1, 2, 4, 128, 256), f)
    sno = np.zeros((16, 1, 2, 4, 128), f)
    smo = np.zeros((16, 1, 2, 4), f)
    for core in range(NCORES):
        o = r[core]
        if core < 2:
            y_sample[core] = o['y']
        else:
            for s, q in enumerate(PROMPT_SEQS[core]):
                y_prompt[q] = o['y'][s * 256:(s + 1) * 256]
                sd[q, 0] = o['st_delta'][s]
                sCo[q, 0] = o['st_C'][s]
                sno[q, 0] = o['st_n'][s]
                smo[q, 0] = o['st_m'][s].reshape(2, 4)
    return (y_prompt, y_sample, sd, sCo, sno, smo)
```

```python
import numpy as np
import concourse.bass as bass
import concourse.mybir as mybir
from concourse.bass_utils import run_bass_kernel_spmd

F32 = mybir.dt.float32
BF16 = mybir.dt.bfloat16
F32R = mybir.dt.float32r
AF = mybir.ActivationFunctionType
ALU = mybir.AluOpType
AX = mybir.AxisListType

ENGS = ('pe', 'act', 'dve', 'pool', 'sp')
SAME_ENGINE_SYNC = {'pe': False, 'act': True, 'dve': True, 'pool': True, 'sp': False}


class Tile:
    def __init__(self, fw, name, h):
        self.fw = fw
        self.name = name
        self.h = h
        self.last_writer = None
        self.readers = {}
        self.dma_sem = None
        self.dma_count = 0
        self.psum = False

    def __getitem__(self, k):
        return self.h[k]

    def ap(self):
        return self.h[:]


class SubTile:
    def __init__(self, parent, ap, name):
        self.__dict__['p'] = parent
        self.__dict__['apv'] = ap
        self.__dict__['name'] = name

    def __getitem__(self, k):
        return self.apv[k]

    def __getattr__(self, k):
        return getattr(self.p, k)

    def __setattr__(self, k, v):
        setattr(self.p, k, v)


class Op:
    __slots__ = ('eng', 'fn', 'deps', 'is_dma', 'sem', 'semval', 'needs_inc', 'tile')

    def __init__(self, eng, fn, is_dma=False):
        self.eng = eng
        self.fn = fn
        self.deps = []
        self.is_dma = is_dma
        self.sem = None
        self.semval = None
        self.needs_inc = False
        self.tile = None


class FW:
    def __init__(self, nc):
        self.nc = nc
        self.ops = []
        self.tiles = []
        self.nsb = 0
        self.scopes = []
        self.debug = False
        self.names = {}
        self.pending_join = None
        self.jt = None

    def sb(self, name, shape, dtype):
        h = self.nc.alloc_sbuf_tensor(name, list(shape), dtype)
        t = Tile(self, name, h)
        t.last_writer = self.pending_join
        self.tiles.append(t)
        return t

    def ps(self, name, shape, dtype=F32):
        h = self.nc.alloc_psum_tensor(name, list(shape), dtype)
        t = Tile(self, name, h)
        t.psum = True
        self.tiles.append(t)
        return t

    def scope_begin(self):
        self.scopes.append([])

    def sbs(self, name, shape, dtype):
        g = self.nc.sbuf_tensor(name, list(shape), dtype)
        h = g.__enter__()
        t = Tile(self, name, h)
        t.last_writer = self.pending_join
        self.tiles.append(t)
        self.scopes[-1].append((g, t))
        return t

    def scope_end(self):
        items = self.scopes.pop()
        jt = self.jt
        j = self.dma('sp', jt[0:1, 1:2], jt[0:1, 0:1], r=[], w=[jt] + [t for g, t in items])
        self.pending_join = j
        for g, t in reversed(items):
            g.__exit__(None, None, None)

    def view(self, name, h):
        t = Tile(self, name, h)
        self.tiles.append(t)
        return t

    def _track(self, o, r, w):
        deps = []
        for t in r:
            if t.last_writer is not None:
                deps.append(t.last_writer)
            if t.psum:
                deps.extend(rd for rd in t.readers.values() if rd.eng != o.eng)
        for t in w:
            if t.last_writer is not None:
                deps.append(t.last_writer)
            deps.extend(t.readers.values())
        seen = set()
        for d in deps:
            if d is o or id(d) in seen:
                continue
            seen.add(id(d))
            if (not d.is_dma) and d.eng == o.eng and not SAME_ENGINE_SYNC[o.eng]:
                continue
            o.deps.append(d)
            d.needs_inc = True
        for t in r:
            t.readers[id(o) if o.is_dma else o.eng] = o
        for t in w:
            t.last_writer = o
            t.readers = {}

    def op(self, eng, fn, r=(), w=()):
        o = Op(eng, fn)
        if self.debug:
            import sys as _s
            f = _s._getframe(1)
            while f is not None and f.f_code.co_name != 'build':
                f = f.f_back
            o.tile = f.f_lineno if f is not None else None
        self._track(o, r, w)
        self.ops.append(o)
        return o

    def dma(self, q, out, in_, r=(), w=(), group=None, **kw):
        o = Op(q, lambda e: e.dma_start(out=out, in_=in_, **kw), is_dma=True)
        o.tile = (list(w) + list(r))[0]
        if group is not None:
            o.deps = list(group.deps)
            for t in w:
                t.last_writer = o
        else:
            self._track(o, r, w)
        o.needs_inc = True
        self.ops.append(o)
        return o

    def emit(self):
        nc = self.nc
        sems = {e: nc.alloc_semaphore(name="s_" + e) for e in ENGS}
        cnt = {e: 0 for e in ENGS}
        for o in self.ops:
            if o.is_dma:
                t = o.tile
                if t.dma_sem is None:
                    t.dma_sem = nc.alloc_semaphore(name="d_" + t.name)
                t.dma_count += 1
                o.sem = t.dma_sem
                o.semval = 16 * t.dma_count
            elif o.needs_inc:
                cnt[o.eng] += 1
                o.sem = sems[o.eng]
                o.semval = cnt[o.eng]
        streams = {e: [o for o in self.ops if o.eng == e] for e in ENGS}
        finals = [(t.dma_sem, 16 * t.dma_count) for t in self.tiles if t.dma_sem is not None]
        self.n_instr = {e: len(streams[e]) for e in ENGS}

        def run(e, eng):
            waited = {}
            for o in streams[e]:
                need = {}
                for d in o.deps:
                    k = id(d.sem)
                    if waited.get(k, (None, 0))[1] >= d.semval:
                        continue
                    if k not in need or need[k][1] < d.semval:
                        need[k] = (d.sem, d.semval)
                for k, (s, v) in need.items():
                    eng.wait_ge(s, v)
                    waited[k] = (s, v)
                ins = o.fn(eng)
                if self.debug:
                    try:
                        self.names[ins.ins.name] = o.tile
                    except Exception:
                        pass
                if o.needs_inc:
                    if o.is_dma:
                        ins.then_inc(o.sem, 16)
                    else:
                        ins.then_inc(o.sem, 1)
            if e == 'sp':
                for s, v in finals:
                    eng.wait_ge(s, v)

        with nc.Block() as block:
            @block.sync
            def _(eng):
                run('sp', eng)

            @block.scalar
            def _(eng):
                run('act', eng)

            @block.vector
            def _(eng):
                run('dve', eng)

            @block.gpsimd
            def _(eng):
                run('pool', eng)

            @block.tensor
            def _(eng):
                run('pe', eng)


D = 2048
KC = 16
NT = 1024
NTILE = 8
NSLOT = 4
FFN = 8192
PW = 7216
EPS = 1e-6
BIG = 30000.0
NCORES = 8


class RP:
    def __init__(self, fw, name, shape, dtype, n, psum=False, scoped=False):
        mk = fw.ps if psum else (fw.sbs if scoped else fw.sb)
        self.t = [mk("%s%d" % (name, i), shape, dtype) for i in range(n)]
        self.i = 0

    def get(self):
        t = self.t[self.i % len(self.t)]
        self.i += 1
        return t


def make_consts():
    p = np.arange(128)
    same = (p[:, None] // 64) == (p[None, :] // 64)
    c = {}
    c['ident'] = np.eye(128, dtype=np.float32)
    c['trif'] = (same & (p[:, None] <= p[None, :])).astype(np.float32)
    c['trib'] = (same & (p[:, None] >= p[None, :])).astype(np.float32)
    c['onesl'] = np.broadcast_to((p[:, None] < 64), (128, 128)).astype(np.float32)
    c['onesr'] = np.broadcast_to((p[:, None] >= 64), (128, 128)).astype(np.float32)
    ustrict = same & (p[None, :] > p[:, None])
    lstrict = same & (p[None, :] < p[:, None])
    uincl = same & (p[None, :] >= p[:, None])
    lincl = same & (p[None, :] <= p[:, None])
    c['maskf'] = np.concatenate([-BIG * (~ustrict), BIG * (~lstrict), -BIG * (~uincl)], axis=1).astype(np.float32)
    c['maskb'] = np.concatenate([-BIG * (~lstrict), BIG * (~ustrict), -BIG * (~lincl)], axis=1).astype(np.float32)
    c['m01f'] = uincl.astype(np.float32)
    c['m01b'] = lincl.astype(np.float32)
    names = ['ident', 'trif', 'trib', 'onesl', 'onesr', 'maskf', 'maskb', 'm01f', 'm01b']
    cat = np.concatenate([c[n] for n in names], axis=1)
    offs = {}
    o = 0
    for n in names:
        offs[n] = (o, c[n].shape[1])
        o += c[n].shape[1]
    sel = np.zeros((16, 16, 128), np.float32)
    for s in range(16):
        sel[s, s, :] = 1.0
    return np.ascontiguousarray(cat), offs, sel.reshape(16, 16 * 128)


def build(stage=99):
    nc = bass.Bass("TRN2", target_bir_lowering=False)
    fw = FW(nc)
    cat, offs, selnp = make_consts()
    NCF = cat.shape[1]

    def din(name, shape):
        return nc.dram_tensor(name, list(shape), F32, kind="ExternalInput").ap()

    def dout(name, shape):
        return nc.dram_tensor(name, list(shape), F32, kind="ExternalOutput").ap()

    xin = din("xin", [NT, D])
    cond = din("cond", [1, D])
    flags = din("flags", [128, 2])
    sdelta = din("sdelta", [2, 8, 128, 128])
    sC = din("sC", [2, 4, 128, 256])
    sn = din("sn", [2, 4, 128])
    sm = din("sm", [1, 8])
    w_ada = din("w_ada", [D, 6 * D])
    b_ada = din("b_ada", [1, 6 * D])
    nrm = din("nrm", [4, D])
    w_in = din("w_in", [D, PW])
    conv_w = din("conv_w", [3072, 3])
    gparams = din("gparams", [1, 48])
    norm_a = din("norm_a", [1, 128])
    norm_b = din("norm_b", [1, 256])
    w_out = din("w_out", [D, D])
    w1 = din("w1", [D, FFN])
    w2 = din("w2", [FFN, D])
    cst = din("cst", [128, NCF])
    y = dout("y", [NT, D])
    st_delta = dout("st_delta", [4, 2, 8, 128, 128])
    st_C = dout("st_C", [4, 2, 4, 128, 256])
    st_n = dout("st_n", [4, 2, 4, 128])
    st_m = dout("st_m", [4, 8])

    dbgt = {}

    def act(out, in_, func, r, w, bias=None, scale=None, accum=None):
        kw = {}
        if bias is not None:
            kw['bias'] = bias
        if scale is not None:
            kw['scale'] = scale
        if accum is not None:
            kw['accum_out'] = accum
        return fw.op('act', lambda e: e.activation(out, in_, func, **kw), r=r, w=w)

    def tt(eng, out, a, b, op, r, w):
        return fw.op(eng, lambda e: e.tensor_tensor(out, a, b, op), r=r, w=w)

    def ts(eng, out, a, s1, s2, op0, op1, r, w):
        if op1 is None:
            return fw.op(eng, lambda e: e.tensor_scalar(out, a, s1, None, op0), r=r, w=w)
        return fw.op(eng, lambda e: e.tensor_scalar(out, a, s1, s2, op0, op1), r=r, w=w)

    def stt(eng, out, a, s, b, op0, op1, r, w):
        return fw.op(eng, lambda e: e.scalar_tensor_tensor(out, a, s, b, op0, op1), r=r, w=w)

    def cp(eng, out, in_, r, w):
        if eng == 'act':
            return fw.op('act', lambda e: e.activation(out, in_, AF.Copy), r=r, w=w)
        return fw.op(eng, lambda e: e.tensor_copy(out, in_), r=r, w=w)

    def mm(out, lhsT, rhs, start, stop, r, w):
        return fw.op('pe', lambda e: e.matmul(out, lhsT, rhs, start=start, stop=stop), r=r, w=w)

    def tr(out, in_, ident, r, w):
        return fw.op('pe', lambda e: e.transpose(out, in_, ident), r=r, w=w)

    def memset(eng, ap, v, w):
        return fw.op(eng, lambda e: e.memset(ap, v), r=[], w=w)

    def recip(out, in_, r, w):
        return fw.op('dve', lambda e: e.reciprocal(out, in_), r=r, w=w)

    fw.jt = fw.sb("jt", [1, 2], F32)
    fw.op('pool', lambda e: e.memset(fw.jt[0:1, 0:2], 0.0), r=[], w=[fw.jt])
    CF = fw.sb("cf", [128, NCF], F32)
    fw.dma('sp', CF[:], cst, w=[CF])

    def cf(name):
        o, n = offs[name]
        return CF[:, o:o + n]


    ONESB = fw.sb("onesb", [128, 128], BF16)
    memset('pool', ONESB[:], 1.0, [ONESB])
    ONE11 = fw.sb("one11", [1, 1], F32)
    memset('pool', ONE11[:], 1.0, [ONE11])
    EPSC = fw.sb("epsc", [128, 2], F32)
    memset('pool', EPSC[:, 0:1], EPS, [EPSC])
    memset('pool', EPSC[:, 1:2], 1.0, [EPSC])
    FLG = fw.sb("flg", [128, 2], F32)
    fw.dma('sp', FLG[:], flags, w=[FLG])

    PBIG = RP(fw, "pbig", [128, 512], F32, 3, psum=True)
    PN = fw.ps("pnorm", [128, 512], F32)
    _banks = [fw.ps("pbank%d" % i, [128, 512], F32) for i in range(4)]
    _subs = [SubTile(_banks[i], _banks[i][:, j * 128:(j + 1) * 128], "psm%d_%d" % (i, j))
             for j in range(4) for i in range(4)]
    PSM = RP.__new__(RP)
    PSM.t = _subs[:15]
    PSM.i = 0
    MODPS = _subs[15]

    WB = RP(fw, "wb", [128, 4096], BF16, 2)
    yT = fw.sb("yT", [128, KC, NT], BF16)
    modT = fw.sb("modT", [128, 96], F32)
    a1 = fw.sb("a1", [128, KC], F32)
    a2 = fw.sb("a2", [128, KC], F32)
    g1p = fw.sb("g1p", [128, KC], F32)
    g2p = fw.sb("g2p", [128, KC], F32)
    NAB = fw.sb("nab", [128, 3], F32)
    STG = RP(fw, "stg", [128, 257], F32, 2)

    def load_w(src, ncol, nk=KC):
        wb = WB.get()
        wv = wb[:, 0:nk * ncol].rearrange("p (k n) -> p k n", k=nk)
        sv = src.rearrange("(k p) n -> p k n", p=128)
        step = max(1, 512 // ncol) if ncol >= 128 else 4
        step = 4
        g = None
        for k4 in range(0, nk, step):
            g2 = fw.dma('pool', wv[:, k4:k4 + step, :], sv[:, k4:k4 + step, :], w=[wb], group=g)
            g = g or g2
        return wb, wv

    fw.scope_begin()
    hT = fw.sbs("hT", [128, KC, NT], BF16)
    condT = fw.sbs("condT", [128, KC], F32)
    badaT = fw.sbs("badaT", [128, 96], F32)
    nrmT = fw.sbs("nrmT", [128, 4, KC], F32)
    rows = fw.sbs("rows", [96, 3, 128], F32)
    sT = fw.sbs("sT", [128, KC], BF16)
    ROWT = [fw.sbs("rowt%d" % i, [1, 256], F32) for i in range(2)]
    fw.dma('sp', rows[0:16, 0, :], cond.rearrange("o (c p) -> (o c) p", p=128), w=[rows])
    fw.dma('sp', rows[0:96, 1, :], b_ada.rearrange("o (c p) -> (o c) p", p=128), w=[rows])
    fw.dma('sp', rows[0:64, 2, :], nrm.rearrange("o (c p) -> (o c) p", p=128), w=[rows])
    for (n, idx, dst) in ((16, 0, condT[:]), (96, 1, badaT[:]), (64, 2, nrmT[:].rearrange("p a c -> p (a c)"))):
        ps = PSM.get()
        tr(ps[:, 0:n], rows[0:n, idx, :], cf('ident')[0:n, 0:n], [rows, CF], [ps])
        cp('dve', dst, ps[:, 0:n], [ps], [condT, badaT, nrmT])
    act(sT[:], condT[:], AF.Silu, [condT], [sT])
    NR = fw.sbs("nr", [3, 128], F32)
    fw.dma('sp', NR[0:1, :], norm_a, w=[NR])
    fw.dma('sp', NR[1:3, :], norm_b.rearrange("o (c p) -> (o c) p", p=128), w=[NR])
    ps = PSM.get()
    tr(ps[:, 0:3], NR[0:3, :], cf('ident')[0:3, 0:3], [NR, CF], [ps])
    cp('dve', NAB[:], ps[:, 0:3], [ps], [NAB])

    rowi = [0]

    def mod_block(nb):
        wb, wv = load_w(w_ada[:, nb * 256:(nb + 1) * 256], 256)
        pb = PBIG.get()
        for kc in range(KC):
            mm(pb[0:1, 0:256], sT[:, kc:kc + 1], wv[:, kc, :], kc == 0, kc == KC - 1, [sT, wb], [pb])
        rt = ROWT[rowi[0] % 2]
        rowi[0] += 1
        cp('act', rt[:], pb[0:1, 0:256], [pb], [rt])
        for j in range(2):
            c = nb * 2 + j
            mm(MODPS[:, c:c + 1], rt[0:1, j * 128:(j + 1) * 128], ONE11[0:1, 0:1], True, True, [rt, ONE11], [MODPS])

    fw.scope_begin()
    CB = fw.sbs("cbf", [128, NCF], BF16)
    fw.dma('pool', CB[:], cst, w=[CB])
    XS = fw.sbs("xs", [128, D], F32)
    XN = fw.sbs("xn", [128, D], BF16)
    SSQ = [fw.sbs("ssq%d" % i, [128, 2], F32) for i in range(2)]
    for t in range(NTILE):
        xs = XS
        fw.dma('sp', xs[:], xin[t * 128:(t + 1) * 128, :], w=[xs])
        xn = XN
        sq = SSQ[t % 2]
        act(xn[:], xs[:], AF.Square, [xs], [xn, sq], accum=sq[:, 0:1])
        act(sq[:, 1:2], sq[:, 0:1], AF.Ln, [sq, EPSC], [sq], scale=1.0 / D, bias=EPSC[:, 0:1])
        act(sq[:, 1:2], sq[:, 1:2], AF.Exp, [sq], [sq], scale=-0.5)
        act(xn[:], xs[:], AF.Copy, [xs, sq], [xn], scale=sq[:, 1:2])
        for c in range(KC):
            ps = PSM.get()
            pv = ps[:].bitcast(BF16)
            tr(pv[:, 0:128], xn[:, c * 128:(c + 1) * 128], CB[:, offs['ident'][0]:offs['ident'][0] + 128], [xn, CB], [ps])
            cp('act' if c % 2 == 0 else 'dve', hT[:, c, t * 128:(t + 1) * 128], pv[:, 0:128], [ps], [hT])
    fw.scope_end()
    for nb in range(16):
        mod_block(nb)
    tt('dve', modT[:, 0:32], MODPS[:, 0:32], badaT[:, 0:32], ALU.add, [MODPS, badaT], [modT])
    stt('dve', a1[:], modT[:, 16:32], 1.0, nrmT[:, 0, :], ALU.add, ALU.mult, [modT, nrmT], [a1])

    for c in range(KC):
        if c % 2 == 0:
            act(hT[:, c, :], hT[:, c, :], AF.Identity, [hT, a1, modT], [hT], scale=a1[:, c:c + 1], bias=modT[:, c:c + 1])
        else:
            ts('dve', hT[:, c, :], hT[:, c, :], a1[:, c:c + 1], modT[:, c:c + 1], ALU.mult, ALU.add,
               [hT, a1, modT], [hT])

    def mod_finish():
        tt('dve', modT[:, 32:96], MODPS[:, 32:96], badaT[:, 32:96], ALU.add, [MODPS, badaT], [modT])
        tt('dve', g1p[:], modT[:, 32:48], nrmT[:, 1, :], ALU.mult, [modT, nrmT], [g1p])
        stt('dve', a2[:], modT[:, 64:80], 1.0, nrmT[:, 2, :], ALU.add, ALU.mult, [modT, nrmT], [a2])
        tt('dve', g2p[:], modT[:, 80:96], nrmT[:, 3, :], ALU.mult, [modT, nrmT], [g2p])

    GW = fw.sbs("gw", [128, KC, 48], BF16)
    for k4 in range(0, KC, 4):
        fw.dma('pool', GW[:, k4:k4 + 4, 0:32],
               w_in[:, 4096:4128].rearrange("(k p) n -> p k n", p=128)[:, k4:k4 + 4, :], w=[GW])
        fw.dma('pool', GW[:, k4:k4 + 4, 32:48],
               w_in[:, 7200:7216].rearrange("(k p) n -> p k n", p=128)[:, k4:k4 + 4, :], w=[GW])
    GPR = fw.sbs("gpr", [128, 48], F32)
    fw.dma('sp', GPR[:], gparams.partition_broadcast(128), w=[GPR])
    negA = fw.sbs("negA", [128, 16], F32)
    act(negA[:], GPR[:, 0:16], AF.Exp, [GPR], [negA])
    ts('dve', negA[:], negA[:], -1.0, None, ALU.mult, None, [negA], [negA])
    SM = fw.sbs("smt", [128, 8], F32)
    fw.dma('sp', SM[:], sm.partition_broadcast(128), w=[SM])
    EM0 = fw.sbs("em0", [128, 8], F32)
    act(EM0[:], SM[:], AF.Exp, [SM], [EM0])
    CWR = fw.sbs("cwr", [24, 384], F32)
    fw.dma('sp', CWR[:], conv_w.rearrange("(c p) j -> c (p j)", p=128), w=[CWR])
    CW = fw.sbs("cw", [128, 3, 24], F32)
    CWF = fw.sbs("cwf", [128, 2, 24], F32)
    cwr3 = CWR[:].rearrange("c (p j) -> c j p", j=3)
    for j in range(3):
        ps = PSM.get()
        tr(ps[:, 0:24], cwr3[0:24, j, :], cf('ident')[0:24, 0:24], [CWR, CF], [ps])
        cp('dve', CW[:, j, :], ps[:, 0:24], [ps], [CW])
    NFL = fw.sbs("nfl", [128, 1], F32)
    ts('dve', NFL[:], FLG[:, 0:1], -1.0, None, ALU.mult, None, [FLG], [NFL])
    ts('dve', CWF[:, 0, :], CW[:, 0, :], NFL[:, 0:1], None, ALU.mult, None, [CW, NFL], [CWF])
    ts('dve', CWF[:, 1, :], CW[:, 2, :], NFL[:, 0:1], None, ALU.mult, None, [CW, NFL], [CWF])

    GX = fw.sbs("gx", [128, NTILE, 64], F32)
    memset('pool', GX[:], 0.0, [GX])
    LB = fw.sbs("lb", [128, NTILE, 16], F32)
    CS = fw.sbs("cs", [128, NTILE, 24], F32)
    TOT = fw.sbs("tot", [128, NTILE, 2, 24], F32)
    EG = fw.sbs("eg", [128, NTILE, 2, 24], F32)
    DS = fw.sbs("ds", [128, NTILE, 104], F32)
    FMF = fw.sbs("fmf", [16, NT], F32)
    FMB = fw.sbs("fmb", [16, NT], F32)
    TMPG = [fw.sbs("tmpg%d" % i, [128, 48], F32) for i in range(2)]
    one_b = EPSC[:, 1:2]
    for t in range(NTILE):
        gp = PBIG.get()
        for kc in range(KC):
            mm(gp[:, 0:48], hT[:, kc, t * 128:(t + 1) * 128], GW[:, kc, :], kc == 0, kc == KC - 1, [hT, GW], [gp])
        tg = TMPG[t % 2]
        tt('dve', tg[:, 0:16], gp[:, 0:16], GPR[:, 16:32], ALU.add, [gp, GPR], [tg])
        act(tg[:, 0:16], tg[:, 0:16], AF.Exp, [tg], [tg])
        act(tg[:, 0:16], tg[:, 0:16], AF.Ln, [tg, EPSC], [tg], bias=one_b)
        act(tg[:, 16:32], gp[:, 16:32], AF.Exp, [gp], [tg], scale=-1.0)
        act(tg[:, 16:32], tg[:, 16:32], AF.Ln, [tg, EPSC], [tg], bias=one_b)
        tt('dve', tg[:, 40:48], gp[:, 40:48], GPR[:, 40:48], ALU.add, [gp, GPR], [tg])
        act(tg[:, 40:48], tg[:, 40:48], AF.Exp, [tg], [tg], scale=-1.0)
        act(tg[:, 40:48], tg[:, 40:48], AF.Ln, [tg, EPSC], [tg], bias=one_b)
        tt('dve', LB[:, t, 8:16], gp[:, 32:40], GPR[:, 32:40], ALU.add, [gp, GPR], [LB])
        tt('dve', GX[:, t, 0:8], tg[:, 0:8], negA[:, 0:8], ALU.mult, [tg, negA], [GX])
        tt('dve', GX[:, t, 8:16], tg[:, 0:8], negA[:, 0:8], ALU.mult, [tg, negA], [GX])
        tt('dve', GX[:, t, 32:40], tg[:, 8:16], negA[:, 8:16], ALU.mult, [tg, negA], [GX])
        tt('dve', GX[:, t, 40:48], tg[:, 8:16], negA[:, 8:16], ALU.mult, [tg, negA], [GX])
        ts('dve', GX[:, t, 24:32], tg[:, 16:24], -1.0, None, ALU.mult, None, [tg], [GX])
        ts('dve', GX[:, t, 56:64], tg[:, 24:32], -1.0, None, ALU.mult, None, [tg], [GX])
        ts('dve', LB[:, t, 0:8], tg[:, 40:48], -1.0, None, ALU.mult, None, [tg], [LB])
        cps = PSM.get()
        mm(cps[:, 0:8], cf('trif'), GX[:, t, 0:8], True, True, [CF, GX], [cps])
        mm(cps[:, 8:16], cf('trib'), GX[:, t, 32:40], True, True, [CF, GX], [cps])
        mm(cps[:, 16:20], cf('trif'), LB[:, t, 0:4], True, True, [CF, LB], [cps])
        mm(cps[:, 20:24], cf('trib'), LB[:, t, 4:8], True, True, [CF, LB], [cps])
        cp('act', CS[:, t, :], cps[:, 0:24], [cps], [CS])
        tps = PSM.get()
        for c, on in enumerate(('onesl', 'onesr')):
            b0 = c * 24
            mm(tps[:, b0:b0 + 8], cf(on), GX[:, t, 0:8], True, True, [CF, GX], [tps])
            mm(tps[:, b0 + 8:b0 + 16], cf(on), GX[:, t, 32:40], True, True, [CF, GX], [tps])
            mm(tps[:, b0 + 16:b0 + 24], cf(on), LB[:, t, 0:8], True, True, [CF, LB], [tps])
        cp('dve', TOT[:, t, :, :].rearrange("p c n -> p (c n)"), tps[:, 0:48], [tps], [TOT])
        act(EG[:, t, :, :].rearrange("p c n -> p (c n)"), tps[:, 0:48], AF.Exp, [tps], [EG])
        fps = PSM.get()
        mm(fps[0:16, :], GX[:, t, 0:16], cf('trif'), True, False, [GX, CF], [fps])
        mm(fps[0:16, :], GX[:, t, 16:32], cf('ident'), False, True, [GX, CF], [fps])
        cp('dve', FMF[0:16, t * 128:(t + 1) * 128], fps[0:16, :], [fps], [FMF])
        fps = PSM.get()
        mm(fps[0:16, :], GX[:, t, 32:48], cf('trib'), True, False, [GX, CF], [fps])
        mm(fps[0:16, :], GX[:, t, 48:64], cf('ident'), False, True, [GX, CF], [fps])
        cp('act', FMB[0:16, t * 128:(t + 1) * 128], fps[0:16, :], [fps], [FMB])
        ts('dve', DS[:, t, 0:16], CS[:, t, 0:16], -1.0, None, ALU.mult, None, [CS], [DS])
        tt('dve', DS[:, t, 16:24], CS[:, t, 0:8], GX[:, t, 24:32], ALU.add, [CS, GX], [DS])
        tt('dve', DS[:, t, 24:32], CS[:, t, 8:16], GX[:, t, 56:64], ALU.add, [CS, GX], [DS])
        act(DS[:, t, 32:40], GX[:, t, 24:32], AF.Exp, [GX], [DS])
        act(DS[:, t, 40:48], GX[:, t, 56:64], AF.Exp, [GX], [DS])
        act(DS[:, t, 48:64], DS[:, t, 16:32], AF.Exp, [DS], [DS])
        tt('dve', DS[:, t, 80:88], LB[:, t, 8:16], CS[:, t, 16:24], ALU.subtract, [LB, CS], [DS])
        for c in range(2):
            po = c * 64
            tt('dve', DS[po:po + 64, t, 64:80], TOT[po:po + 64, t, c, 0:16], CS[po:po + 64, t, 0:16], ALU.subtract,
               [TOT, CS], [DS])
            tt('dve', DS[po:po + 64, t, 96:104], DS[po:po + 64, t, 80:88], TOT[po:po + 64, t, c, 16:24], ALU.add,
               [TOT, DS], [DS])
        act(DS[:, t, 64:80], DS[:, t, 64:80], AF.Exp, [DS], [DS])
        act(DS[:, t, 80:88], DS[:, t, 80:88], AF.Exp, [DS], [DS])
        act(DS[:, t, 96:104], DS[:, t, 96:104], AF.Exp, [DS], [DS])
        act(DS[:, t, 88:96], CS[:, t, 16:24], AF.Exp, [CS], [DS])
    scr = nc.dram_tensor("scr_fm", [32, NT], F32).ap()
    SCRt = fw.view("scrfm", scr)
    fw.dma('sp', scr[0:16, :], FMF[:], r=[FMF], w=[SCRt])
    fw.dma('sp', scr[16:32, :], FMB[:], r=[FMB], w=[SCRt])
    EMF = fw.sbs("emf", [128, 32], F32)
    fw.scope_begin()
    LFT = fw.sbs("lft", [8, NT], F32)
    GMT = fw.sbs("gmt", [8, NT], F32)
    GTMS = fw.sbs("gtms", [128, NTILE, 8], F32)
    GOFF = fw.sbs("goff", [128, 8], F32)
    GTP = fw.sbs("gtp", [128, 8], F32)
    for slot in range(4):
        chunks = [(2 * slot, 0), (2 * slot, 1), (2 * slot + 1, 0), (2 * slot + 1, 1)]
        for d in range(2):
            order = chunks if d == 0 else chunks[::-1]
            cs0, lb0, tc0 = 16 + 4 * d, 8 + 4 * d, 16 + 4 * d
            oc = slice(4 * d, 4 * d + 4)
            for qi, (t, c) in enumerate(order):
                po = c * 64
                if qi == 0:
                    tt('dve', GTMS[po:po + 64, t, oc], LB[po:po + 64, t, lb0:lb0 + 4], CS[po:po + 64, t, cs0:cs0 + 4],
                       ALU.subtract, [LB, CS], [GTMS])
                else:
                    tp, cprev = order[qi - 1]
                    if qi == 1:
                        cp('pool', GOFF[:, oc], TOT[:, tp, cprev, tc0:tc0 + 4], [TOT], [GOFF])
                    else:
                        tt('pool', GOFF[:, oc], GOFF[:, oc], TOT[:, tp, cprev, tc0:tc0 + 4], ALU.add, [GOFF, TOT], [GOFF])
                    tt('dve', GTP[po:po + 64, oc], CS[po:po + 64, t, cs0:cs0 + 4], GOFF[po:po + 64, oc], ALU.add,
                       [CS, GOFF], [GTP])
                    tt('dve', GTMS[po:po + 64, t, oc], LB[po:po + 64, t, lb0:lb0 + 4], GTP[po:po + 64, oc],
                       ALU.subtract, [LB, GTP], [GTMS])
    for t in range(NTILE):
        ps = PSM.get()
        mm(ps[0:8, :], GTMS[:, t, :], cf('ident'), True, True, [GTMS, CF], [ps])
        cp('dve', GMT[0:8, t * 128:(t + 1) * 128], ps[0:8, :], [ps], [GMT])
        ps = PSM.get()
        mm(ps[0:8, :], LB[:, t, 0:8], cf('ident'), True, True, [LB, CF], [ps])
        cp('act', LFT[0:8, t * 128:(t + 1) * 128], ps[0:8, :], [ps], [LFT])
    MF = fw.sbs("mf", [8, 4], F32)
    MFx = fw.sbs("mfx", [8, 4], F32)
    fw.op('dve', lambda e: e.tensor_reduce(MFx[:], GMT[:].rearrange("r (s t) -> r s t", s=4), AX.X, ALU.max),
          r=[GMT], w=[MFx])
    fw.op('dve', lambda e: e.tensor_reduce(MF[:], LFT[:].rearrange("r (s t) -> r s t", s=4), AX.X, ALU.add),
          r=[LFT], w=[MF])
    ts('dve', MFx[:], MFx[:], 0.0, None, ALU.max, None, [MFx], [MFx])
    tt('dve', MF[:], MF[:], MFx[:], ALU.add, [MF, MFx], [MF])
    fw.dma('sp', st_m.rearrange("s r -> r s"), MF[:], r=[MF], allow_slow_non_contiguous=True)
    XD = fw.sbs("xd", [8, 8, 4], F32)
    EMFs = fw.sbs("emfs", [8, 4], F32)
    act(EMFs[:], MF[:], AF.Exp, [MF], [EMFs], scale=-1.0)
    for sl in range(4):
        ts('dve', XD[:, :, sl], cf('ident')[0:8, 0:8], EMFs[:, sl:sl + 1], None, ALU.mult, None, [CF, EMFs], [XD])
    ONES8 = fw.sbs("ones8", [8, 128], F32)
    memset('pool', ONES8[:], 1.0, [ONES8])
    ps = PSM.get()
    mm(ps[:, 0:32], ONES8[:], XD[:].rearrange("r a s -> r (a s)"), True, True, [ONES8, XD], [ps])
    cp('dve', EMF[:], ps[:, 0:32], [ps], [EMF])
    fw.scope_end()

    def proj128(col0, evac):
        wb, wv = load_w(w_in[:, col0:col0 + 128], 128)
        for tb in range(2):
            pb = PBIG.get()
            for kc in range(KC):
                mm(pb[:], wv[:, kc, :], hT[:, kc, tb * 512:(tb + 1) * 512], kc == 0, kc == KC - 1, [wb, hT], [pb])
            evac(pb, tb)

    def slot_of(t):
        return t // 2

    def run_rr(gens):
        gens = list(gens)
        while gens:
            for g in list(gens):
                try:
                    next(g)
                except StopIteration:
                    gens.remove(g)

    fw.scope_begin()
    RAW = fw.sbs("raw", [128, NT], F32)
    GRAWs = [fw.sbs("graw%d" % i, [128, NT], BF16) for i in range(2)]
    Y3 = [fw.sbs("ycv%d" % i, [128, NT], F32) for i in range(3)]
    SQ = fw.sbs("sqb", [128, NT], BF16)
    RVT = fw.sbs("rvt", [128, NT], F32)
    QNs = [fw.sbs("qn%d" % i, [128, NT], BF16) for i in range(2)]
    KNBs = [fw.sbs("knb%d" % i, [128, NT], BF16) for i in range(2)]
    KTs = [fw.sbs("kt%d" % i, [128, NTILE, 128], BF16) for i in range(2)]
    VTs = [fw.sbs("vt%d" % i, [128, NTILE, 128], BF16) for i in range(2)]
    OO = [fw.sbs("oo%d" % i, [128, NTILE, 128], F32) for i in range(2)]
    OSS = fw.sbs("oss", [128, NTILE], F32)
    JNK = fw.sbs("jnk", [128, 128], BF16)
    UG = [RP(fw, "ug%d_" % i, [128, 128], F32, 4, scoped=True) for i in range(2)]
    BC = [RP(fw, "bc%d_" % i, [128, 2, 128], F32, 2, scoped=True) for i in range(2)]
    CH = [[RP(fw, "ch%d_%d_" % (i, j), [128, 128], F32, 8, scoped=True) for j in range(2)] for i in range(2)]
    KK = [RP(fw, "kk%d_" % i, [128, 128], F32, 1, scoped=True) for i in range(2)]
    QK = [RP(fw, "qk%d_" % i, [128, 128], F32, 1, scoped=True) for i in range(2)]
    UU = [RP(fw, "uu%d_" % i, [128, 128], F32, 3, scoped=True) for i in range(2)]
    TTB = [RP(fw, "ttb%d_" % i, [128, 128], BF16, 2, scoped=True) for i in range(2)]
    WT = [RP(fw, "wt%d_" % i, [128, 128], BF16, 3, scoped=True) for i in range(2)]
    ATT = [RP(fw, "att%d_" % i, [128, 128], BF16, 3, scoped=True) for i in range(2)]
    QG = [RP(fw, "qg%d_" % i, [128, 128], BF16, 3, scoped=True) for i in range(2)]
    KBG = [RP(fw, "kbg%d_" % i, [128, 128], BF16, 2, scoped=True) for i in range(2)]
    KD = [RP(fw, "kd%d_" % i, [128, 128], BF16, 3, scoped=True) for i in range(2)]
    VBt = [RP(fw, "vbt%d_" % i, [128, 128], BF16, 2, scoped=True) for i in range(2)]
    VN = [RP(fw, "vn%d_" % i, [128, 128], BF16, 2, scoped=True) for i in range(2)]
    SA = [fw.sbs("sa%d" % i, [128, 128], F32) for i in range(2)]
    SAb = [fw.sbs("sab%d" % i, [128, 128], BF16) for i in range(2)]

    PRE_PS = [[[PSM.t[(dd * 2 + par) * 3 + j] for j in range(3)] for par in range(2)] for dd in range(2)]
    SCAN_PS = [[SubTile(PBIG.t[dd], PBIG.t[dd][:, j * 128:(j + 1) * 128], "pscan%d_%d" % (dd, j)) for j in range(3)]
               for dd in range(2)]
    PSS = RP.__new__(RP)
    PSS.t = PSM.t[12:15]
    PSS.i = 0

    def conv_silu_gen(dst, ci):
        x3 = RAW[:].rearrange("p (s t) -> p s t", s=4)
        y3 = dst[:].rearrange("p (s t) -> p s t", s=4)
        x4 = RAW[:].rearrange("p (s c t) -> p s c t", s=4, c=4)
        y4 = dst[:].rearrange("p (s c t) -> p s c t", s=4, c=4)
        act(dst[:], RAW[:], AF.Copy, [RAW, CW], [dst], scale=CW[:, 1, ci:ci + 1])
        yield
        stt('dve', y3[:, :, 1:256], x3[:, :, 0:255], CW[:, 0, ci:ci + 1], y3[:, :, 1:256], ALU.mult, ALU.add,
            [RAW, CW, dst], [dst])
        yield
        stt('dve', y3[:, :, 0:255], x3[:, :, 1:256], CW[:, 2, ci:ci + 1], y3[:, :, 0:255], ALU.mult, ALU.add,
            [RAW, CW, dst], [dst])
        yield
        stt('dve', y4[:, :, 1:4, 0], x4[:, :, 0:3, 63], CWF[:, 0, ci:ci + 1], y4[:, :, 1:4, 0], ALU.mult, ALU.add,
            [RAW, CWF, dst], [dst])
        stt('dve', y4[:, :, 0:3, 63], x4[:, :, 1:4, 0], CWF[:, 1, ci:ci + 1], y4[:, :, 0:3, 63], ALU.mult, ALU.add,
            [RAW, CWF, dst], [dst])
        yield
        act(dst[:], dst[:], AF.Silu, [dst], [dst])
        yield

    def rinv_gen(src):
        tt('pool', SQ[:], src[:], src[:], ALU.mult, [src], [SQ])
        yield
        for tb in range(2):
            mm(PN[:], ONESB[:], SQ[:, tb * 512:(tb + 1) * 512], True, True, [ONESB, SQ], [PN])
            yield
            act(RVT[:, tb * 512:(tb + 1) * 512], PN[:], AF.Ln, [PN, EPSC], [RVT], bias=EPSC[:, 0:1])
            yield
        act(RVT[:], RVT[:], AF.Exp, [RVT], [RVT], scale=-0.5)
        yield

    def proj_gen(col0, evac):
        wb, wv = load_w(w_in[:, col0:col0 + 128], 128)
        pb = PBIG.t[2]
        for tb in range(2):
            for kc in range(KC):
                mm(pb[:], wv[:, kc, :], hT[:, kc, tb * 512:(tb + 1) * 512], kc == 0, kc == KC - 1, [wb, hT], [pb])
                if kc % 4 == 3:
                    yield
            evac(pb, tb)
            yield

    def mod_gen(nb):
        wb, wv = load_w(w_ada[:, nb * 256:(nb + 1) * 256], 256)
        pb = PBIG.t[2]
        for kc in range(KC):
            mm(pb[0:1, 0:256], sT[:, kc:kc + 1], wv[:, kc, :], kc == 0, kc == KC - 1, [sT, wb], [pb])
            if kc % 4 == 3:
                yield
        rt = ROWT[rowi[0] % 2]
        rowi[0] += 1
        cp('act', rt[:], pb[0:1, 0:256], [pb], [rt])
        yield
        for j in range(2):
            c = nb * 2 + j
            mm(MODPS[:, c:c + 1], rt[0:1, j * 128:(j + 1) * 128], ONE11[0:1, 0:1], True, True, [rt, ONE11], [MODPS])
        yield

    def a_prologue(h):
        b = h % 2
        QN, KNB, KT, VT, GRAW = QNs[b], KNBs[b], KTs[b], VTs[b], GRAWs[b]

        def ev_raw(pb, tb):
            cp('act' if tb == 0 else 'dve', RAW[:, tb * 512:(tb + 1) * 512], pb[:], [pb], [RAW])

        def ev_gate(pb, tb):
            act(GRAW[:, tb * 512:(tb + 1) * 512], pb[:], AF.Silu, [pb], [GRAW])

        for i, base in enumerate((0, 1024, 2048)):
            yield from proj_gen(base + h * 128, ev_raw)
            yield from conv_silu_gen(Y3[i], i * 8 + h)
        yield from proj_gen(3072 + h * 128, ev_gate)
        if h >= 1:
            for nb in range(16 + 4 * (h - 1), 20 + 4 * (h - 1)):
                yield from mod_gen(nb)
        yield from rinv_gen(Y3[0])
        stt('dve', QN[:], Y3[0][:], float(128 ** -0.5), RVT[:], ALU.mult, ALU.mult, [Y3[0], RVT], [QN])
        yield
        yield from rinv_gen(Y3[1])
        tt('dve', Y3[1][:], Y3[1][:], RVT[:], ALU.mult, [Y3[1], RVT], [Y3[1]])
        yield
        cp('pool', KNB[:], Y3[1][:], [Y3[1]], [KNB])
        yield
        for t in range(NTILE):
            ps = PSS.get()
            tr(ps[:], Y3[1][:, t * 128:(t + 1) * 128], cf('ident'), [Y3[1], CF], [ps])
            cp('act', KT[:, t, :], ps[:], [ps], [KT])
            yield
            ps = PSS.get()
            tr(ps[:], Y3[2][:, t * 128:(t + 1) * 128], cf('ident'), [Y3[2], CF], [ps])
            cp('dve', VT[:, t, :], ps[:], [ps], [VT])
            yield

    def run_bg(gens, bg):
        gens = list(gens)
        while gens:
            for g in list(gens):
                try:
                    next(g)
                except StopIteration:
                    gens.remove(g)
            if bg[0] is not None:
                try:
                    next(bg[0])
                except StopIteration:
                    bg[0] = None

    pre_out = {}
    pre_g = {}

    def a_pre(h, s, d):
        QN, KNB, KT, VT = QNs[h % 2], KNBs[h % 2], KTs[h % 2], VTs[h % 2]
        t = s if d == 0 else NTILE - 1 - s
        col = d * 8 + h
        tok = slice(t * 128, (t + 1) * 128)
        CHp = CH[d][s % 2]
        DPS = PRE_PS[d][s % 2]
        bc = BC[d].get()
        g0 = fw.dma('sp', bc[:, 0, :], scr[16 * d + 8 + h:16 * d + 9 + h, tok].partition_broadcast(128),
                    r=[SCRt], w=[bc])
        fw.dma('sp', bc[:, 1, :], scr[16 * d + h:16 * d + h + 1, tok].partition_broadcast(128),
               r=[SCRt], w=[bc], group=g0)
        ps = PSS.get()
        mm(ps[:], KNB[:, tok], KNB[:, tok], True, True, [KNB], [ps])
        kk = KK[d].get()
        cp('act', kk[:], ps[:], [ps], [kk])
        ps = PSS.get()
        mm(ps[:], KNB[:, tok], QN[:, tok], True, True, [KNB, QN], [ps])
        qk = QK[d].get()
        cp('dve', qk[:], ps[:], [ps], [qk])
        mo, _ = offs['maskf' if d == 0 else 'maskb']
        gM, gL, e1, egt = UG[d].get(), UG[d].get(), UG[d].get(), UG[d].get()
        tt('dve', gM[:], bc[:, 0, :], CF[:, mo:mo + 128], ALU.add, [bc, CF], [gM])
        tt('dve', gL[:], bc[:, 1, :], CF[:, mo + 128:mo + 256], ALU.add, [bc, CF], [gL])
        tt('dve', e1[:], bc[:, 1, :], CF[:, mo + 256:mo + 384], ALU.add, [bc, CF], [e1])
        yield
        act(gM[:], gM[:], AF.Exp, [gM, DS], [gM], bias=DS[:, t, col:col + 1])
        act(gL[:], gL[:], AF.Exp, [gL, DS], [gL], bias=DS[:, t, 16 + col:17 + col], scale=-1.0)
        act(e1[:], e1[:], AF.Exp, [e1, DS], [e1], bias=DS[:, t, col:col + 1])
        act(egt[:], bc[:, 1, :], AF.Exp, [bc], [egt])
        kbg, kd, vb = KBG[d].get(), KD[d].get(), VBt[d].get()
        act(kbg[:], KT[:, t, :], AF.Copy, [KT, DS], [kbg], scale=DS[:, t, 48 + col:49 + col])
        ts('dve', kd[:], KT[:, t, :], DS[:, t, 64 + col:65 + col], None, ALU.mult, None, [KT, DS], [kd])
        act(vb[:], VT[:, t, :], AF.Copy, [VT, DS], [vb], scale=DS[:, t, 32 + col:33 + col])
        yield
        M, L = CHp.get(), CHp.get()
        tt('pool', M[:].bitcast(F32R), kk[:], gM[:], ALU.mult, [kk, gM], [M])
        tt('pool', L[:].bitcast(F32R), kk[:], gL[:], ALU.mult, [kk, gL], [L])
        att = ATT[d].get()
        tt('dve', att[:], qk[:], e1[:], ALU.mult, [qk, e1], [att])
        qg = QG[d].get()
        tt('pool', qg[:], QN[:, tok], egt[:], ALU.mult, [QN, egt], [qg])
        R = CHp.get()
        tt('pool', R[:].bitcast(F32R), cf('ident'), M[:], ALU.subtract, [CF, M], [R])
        yield
        P, Q = M, L
        rps = None
        Qprev = None
        for k in range(1, 7):
            if k <= 5:
                qps = DPS[0]
                mm(qps[:], P[:].bitcast(F32R), Q[:].bitcast(F32R), True, True, [P, Q], [qps])
                if k < 5:
                    pps = DPS[1]
                    mm(pps[:], Q[:].bitcast(F32R), P[:].bitcast(F32R), True, True, [P, Q], [pps])
            if k >= 2:
                rps = DPS[2]
                mm(rps[:], Q[:].bitcast(F32R), R[:].bitcast(F32R), True, True, [Q, R], [rps])
            yield
            if k == 6:
                ttb = TTB[d].get()
                tt('dve', ttb[:], R[:], rps[:], ALU.add, [R, rps], [ttb])
            elif k >= 2:
                Rn = CHp.get()
                tt('dve', Rn[:].bitcast(F32R), R[:], rps[:], ALU.add, [R, rps], [Rn])
                R = Rn
            if k <= 5:
                Qn = CHp.get()
                cp('act', Qn[:].bitcast(F32R), qps[:], [qps], [Qn])
                if k < 5:
                    Pn = CHp.get()
                    cp('dve', Pn[:].bitcast(F32R), pps[:], [pps], [Pn])
                else:
                    Pn = None
                P, Q = Pn, Qn
            yield
        ups = DPS[0]
        mm(ups[:], ttb[:], vb[:], True, True, [ttb, vb], [ups])
        wps = DPS[1]
        mm(wps[:], kbg[:], ttb[:], True, True, [kbg, ttb], [wps])
        yield
        uu = UU[d].get()
        cp('act', uu[:], ups[:], [ups], [uu])
        wt = WT[d].get()
        cp('dve', wt[:], wps[:], [wps], [wt])
        pre_out[(h, s, d)] = (uu, wt, att, qg, kd)


    bg = [a_prologue(0)]
    run_bg([], bg)
    while bg[0] is not None:
        run_rr([bg[0]])
        bg[0] = None
    for h in range(8):
        QN, KNB, KT, VT, GRAW = QNs[h % 2], KNBs[h % 2], KTs[h % 2], VTs[h % 2], GRAWs[h % 2]
        bg = [a_prologue(h + 1) if h + 1 < 8 else None]
        def a_scan(s, d):
            t = s if d == 0 else NTILE - 1 - s
            col = d * 8 + h
            uu, wt, att, qg, kd = pre_out.pop((h, s, d))
            S, Sb = SA[d], SAb[d]
            for ci in range(2):
                c = ci if d == 0 else 1 - ci
                po = c * 64
                first_in_slot = (t % 2 == 0 and c == 0) if d == 0 else (t % 2 == 1 and c == 1)
                last_in_slot = (t % 2 == 1 and c == 1) if d == 0 else (t % 2 == 0 and c == 0)
                if first_in_slot:
                    if s == 0:
                        fw.dma('sp', S[:], sdelta[d, h], w=[S])
                    else:
                        ts('dve', S[:], S[:], FLG[:, 0:1], None, ALU.mult, None, [S, FLG], [S])
                    cp('act', Sb[:], S[:], [S], [Sb])
                    yield
                wsp = SCAN_PS[d][0]
                mm(wsp[:], wt[:], Sb[:], True, True, [wt, Sb], [wsp])
                yield
                vn = VN[d].get()
                tt('dve', vn[po:po + 64, :], uu[po:po + 64, :], wsp[po:po + 64, :], ALU.subtract, [uu, wsp], [vn])
                yield
                ops_ = SCAN_PS[d][1]
                mm(ops_[:], qg[:], Sb[:], True, False, [qg, Sb], [ops_])
                mm(ops_[:], att[po:po + 64, :], vn[po:po + 64, :], False, True, [att, vn], [ops_])
                kvp = SCAN_PS[d][2]
                mm(kvp[:], kd[po:po + 64, :], vn[po:po + 64, :], True, True, [kd, vn], [kvp])
                yield
                cp('act', OO[d][po:po + 64, t, :], ops_[po:po + 64, :], [ops_], [OO[d]])
                stt('dve', Sb[:], S[:], EG[:, t, c, col:col + 1], kvp[:], ALU.mult, ALU.add, [S, EG, kvp], [Sb])
                stt('dve', S[:], S[:], EG[:, t, c, col:col + 1], kvp[:], ALU.mult, ALU.add, [S, EG, kvp], [S])
                if last_in_slot:
                    sg = STG.get()
                    cp('pool', sg[:, 0:128], S[:], [S], [sg])
                    fw.dma('sp', st_delta[slot_of(t), d, h], sg[:, 0:128], r=[sg])
                yield

        def start_pre(s):
            if s < NTILE and (h, s) not in pre_g:
                pre_g[(h, s)] = [a_pre(h, s, 0), a_pre(h, s, 1)]

        def run_multi(fg, bgs):
            fg = list(fg)
            while fg:
                for g in list(fg):
                    try:
                        next(g)
                    except StopIteration:
                        fg.remove(g)
                for g in list(bgs):
                    try:
                        next(g)
                    except StopIteration:
                        bgs.remove(g)
                for _ in range(2):
                    if bg[0] is not None:
                        try:
                            next(bg[0])
                        except StopIteration:
                            bg[0] = None

        if (h, 0) not in pre_g:
            start_pre(0)
            for _ in range(8):
                for g in pre_g[(h, 0)]:
                    next(g)
        start_pre(1)
        run_multi(pre_g[(h, 0)], pre_g[(h, 1)])
        def a_epi(t):
            tk_ = slice(t * 128, (t + 1) * 128)
            tt('pool', OO[0][:, t, :], OO[0][:, t, :], OO[1][:, t, :], ALU.add, [OO[0], OO[1]], [OO[0]])
            yield
            act(JNK[:], OO[0][:, t, :], AF.Square, [OO[0]], [JNK, OSS], accum=OSS[:, t:t + 1])
            yield
            act(OSS[:, t:t + 1], OSS[:, t:t + 1], AF.Ln, [OSS, EPSC], [OSS], scale=1.0 / 128, bias=EPSC[:, 0:1])
            act(OSS[:, t:t + 1], OSS[:, t:t + 1], AF.Exp, [OSS], [OSS], scale=-0.5)
            yield
            ts('dve', OO[0][:, t, :], OO[0][:, t, :], OSS[:, t:t + 1], None, ALU.mult, None, [OO[0], OSS], [OO[0]])
            yield
            ps = PSS.get()
            tr(ps[:], OO[0][:, t, :], cf('ident'), [OO[0], CF], [ps])
            stt('dve', yT[:, h, tk_], ps[:], NAB[:, 0:1], GRAW[:, tk_], ALU.mult, ALU.mult, [ps, NAB, GRAW], [yT])
            yield

        epis = []
        nxt = []
        for s in range(NTILE):
            start_pre(s + 2)
            if s >= 6 and h + 1 < 8:
                if bg[0] is not None:
                    run_rr([bg[0]])
                    bg[0] = None
                pre_g[(h + 1, s - 6)] = [a_pre(h + 1, s - 6, 0), a_pre(h + 1, s - 6, 1)]
                nxt = nxt + pre_g[(h + 1, s - 6)]
            run_multi([a_scan(s, 0), a_scan(s, 1)] + pre_g.get((h, s + 1), []), pre_g.get((h, s + 2), []) + epis + nxt)
            if s >= 4:
                epis = epis + [a_epi(s), a_epi(NTILE - 1 - s)]
        if bg[0] is not None:
            run_rr([bg[0]])
        run_rr(epis)
        if h == 7:
            for nb in range(44, 48):
                mod_block(nb)
            mod_finish()
        if stage == 2:
            def dump(name, ap, shape):
                o = dout("dbg_" + name, shape)
                fw.dma('sp', o, ap, r=list(fw.tiles))
            dump("GX", GX[:].rearrange("p a b -> p (a b)"), [128, NTILE * 64])
            dump("LB", LB[:].rearrange("p a b -> p (a b)"), [128, NTILE * 16])
            dump("CS", CS[:].rearrange("p a b -> p (a b)"), [128, NTILE * 24])
            dump("DS", DS[:].rearrange("p a b -> p (a b)"), [128, NTILE * 104])
            dump("TOT", TOT[:].rearrange("p a b c -> p (a b c)"), [128, NTILE * 48])
            dump("FMF", FMF[:], [16, NT])
            dump("KN", Y3[1][:], [128, NT])
            dump("V", Y3[2][:], [128, NT])
            dump("Q", Y3[0][:], [128, NT])
            dump("RVT", RVT[:], [128, NT])
            dump("OOf", OO[0][:].rearrange("p a b -> p (a b)"), [128, NTILE * 128])
            dump("OOb", OO[1][:].rearrange("p a b -> p (a b)"), [128, NTILE * 128])
            fw.emit()
            return nc, fw
    fw.scope_end()
    fw.scope_begin()
    QB = fw.sbs("qb", [128, NT], BF16)
    KBF = fw.sbs("kbf", [128, NT], F32)
    KBB = fw.sbs("kbb", [128, NT], BF16)
    VBR = fw.sbs("vbr", [128, 2, NT], F32)
    OG = fw.sbs("og", [128, 2, NT], F32)
    KTB = fw.sbs("ktb", [128, NTILE, 128], F32)
    VE = fw.sbs("ve", [128, NTILE, 264], BF16)
    memset('pool', VE[:, :, 256:257], 1.0, [VE])
    HB = [fw.sbs("hb%d" % i, [128, NTILE, 256], F32) for i in range(2)]
    HSS = fw.sbs("hss", [128, NTILE], F32)
    JNK2 = fw.sbs("jnk2", [128, 256], BF16)
    KQ = RP(fw, "kq", [128, 128], F32, 2, scoped=True)
    DW = [RP(fw, "dw%d_" % i, [128, 128], BF16, 2, scoped=True) for i in range(2)]
    KP = [RP(fw, "kp%d_" % i, [128, 128], BF16, 2, scoped=True) for i in range(2)]
    TMB = [RP(fw, "tmb%d_" % i, [128, 2], F32, 2, scoped=True) for i in range(2)]
    CN = [fw.sbs("cn%d" % i, [128, 264], F32) for i in range(2)]
    CNb = [fw.sbs("cnb%d" % i, [128, 264], BF16) for i in range(2)]

    def evac_q(pb, tb):
        act(QB[:, tb * 512:(tb + 1) * 512], pb[:], AF.Copy, [pb], [QB], scale=float(128 ** -0.5))

    def evac_k(pb, tb):
        cp('dve', KBF[:, tb * 512:(tb + 1) * 512], pb[:], [pb], [KBF])
        cp('act', KBB[:, tb * 512:(tb + 1) * 512], pb[:], [pb], [KBB])

    def evac_v(j):
        def f(pb, tb):
            cp('dve' if tb else 'act', VBR[:, j, tb * 512:(tb + 1) * 512], pb[:], [pb], [VBR])
        return f

    def evac_o(j):
        def f(pb, tb):
            act(OG[:, j, tb * 512:(tb + 1) * 512], pb[:], AF.Sigmoid, [pb], [OG])
        return f

    for hb in range(4):
        proj128(4128 + hb * 128, evac_q)
        proj128(4640 + hb * 128, evac_k)
        for j in range(2):
            proj128(5152 + hb * 256 + j * 128, evac_v(j))
            proj128(6176 + hb * 256 + j * 128, evac_o(j))
        for t in range(NTILE):
            tok = slice(t * 128, (t + 1) * 128)
            ps = PSM.get()
            tr(ps[:], KBF[:, tok], cf('ident'), [KBF, CF], [ps])
            cp('act', KTB[:, t, :], ps[:], [ps], [KTB])
            for j in range(2):
                ps = PSM.get()
                tr(ps[:], VBR[:, j, tok], cf('ident'), [VBR, CF], [ps])
                cp('dve', VE[:, t, j * 128:(j + 1) * 128], ps[:], [ps], [VE])
        def b_pre(s, d):
            t = s if d == 0 else NTILE - 1 - s
            col = d * 4 + hb
            tok = slice(t * 128, (t + 1) * 128)
            ps = PSM.get()
            mm(ps[:], KBB[:, tok], QB[:, tok], True, True, [KBB, QB], [ps])
            yield
            dw = DW[d].get()
            stt('dve', dw[:], ps[:], DS[:, t, 80 + col:81 + col], cf('m01f' if d == 0 else 'm01b'),
                ALU.mult, ALU.mult, [ps, DS, CF], [dw])
            kp = KP[d].get()
            act(kp[:], KTB[:, t, :], AF.Copy, [KTB, DS], [kp], scale=DS[:, t, 96 + col:97 + col])
            bpre_out[(s, d)] = (dw, kp)

        def b_scan(s, d):
            t = s if d == 0 else NTILE - 1 - s
            col = d * 4 + hb
            tok = slice(t * 128, (t + 1) * 128)
            dw, kp = bpre_out.pop((s, d))
            C, Cb = CN[d], CNb[d]
            for ci in range(2):
                c = ci if d == 0 else 1 - ci
                po = c * 64
                first_in_slot = (t % 2 == 0 and c == 0) if d == 0 else (t % 2 == 1 and c == 1)
                last_in_slot = (t % 2 == 1 and c == 1) if d == 0 else (t % 2 == 0 and c == 0)
                if first_in_slot:
                    if s == 0:
                        fw.dma('sp', C[:, 0:256], sC[d, hb], w=[C])
                        fw.dma('sp', C[:, 256:257], sn[d, hb, :].rearrange("(p o) -> p o", o=1), w=[C])
                        ts('dve', C[:, 0:257], C[:, 0:257], EM0[:, col:col + 1], None, ALU.mult, None,
                           [C, EM0], [C])
                    else:
                        ts('dve', C[:, 0:257], C[:, 0:257], FLG[:, 0:1], None, ALU.mult, None, [C, FLG], [C])
                    cp('act', Cb[:, 0:257], C[:, 0:257], [C], [Cb])
                    yield
                nd = PBIG.t[d]
                mm(nd[:, 0:257], QB[:, tok], Cb[:, 0:257], True, False, [QB, Cb], [nd])
                mm(nd[:, 0:257], dw[po:po + 64, :], VE[po:po + 64, t, 0:257], False, True, [dw, VE], [nd])
                dc = PBIG.t[2] if d == 0 else PN
                mm(dc[:, 0:257], kp[po:po + 64, :], VE[po:po + 64, t, 0:257], True, True, [kp, VE], [dc])
                yield
                tm = TMB[d].get()
                act(tm[po:po + 64, 0:1], nd[po:po + 64, 256:257], AF.Abs, [nd, DS], [tm],
                    scale=DS[po:po + 64, t, 88 + col:89 + col])
                stt('dve', Cb[:, 0:257], C[:, 0:257], EG[:, t, c, 16 + col:17 + col], dc[:, 0:257],
                    ALU.mult, ALU.add, [C, EG, dc], [Cb])
                stt('dve', C[:, 0:257], C[:, 0:257], EG[:, t, c, 16 + col:17 + col], dc[:, 0:257],
                    ALU.mult, ALU.add, [C, EG, dc], [C])
                ts('dve', tm[po:po + 64, 0:1], tm[po:po + 64, 0:1], 1.0, None, ALU.max, None, [tm], [tm])
                recip(tm[po:po + 64, 0:1], tm[po:po + 64, 0:1], [tm], [tm])
                tt('dve', tm[po:po + 64, 1:2], tm[po:po + 64, 0:1], DS[po:po + 64, t, 88 + col:89 + col], ALU.mult,
                   [tm, DS], [tm])
                yield
                act(HB[d][po:po + 64, t, :], nd[po:po + 64, 0:256], AF.Copy, [nd, tm], [HB[d]],
                    scale=tm[po:po + 64, 1:2])
                if last_in_slot:
                    sl = slot_of(t)
                    sg = STG.get()
                    ts('dve', sg[:, 0:257], C[:, 0:257], EMF[:, col * 4 + sl:col * 4 + sl + 1], None, ALU.mult,
                       None, [C, EMF], [sg])
                    fw.dma('sp', st_C[sl, d, hb], sg[:, 0:256], r=[sg])
                    fw.dma('sp', st_n[sl, d, hb, :].rearrange("(p o) -> p o", o=1), sg[:, 256:257], r=[sg])
                yield

        bpre_out = {}
        run_rr([b_pre(0, 0), b_pre(0, 1)])
        for s in range(NTILE):
            gens = [b_scan(s, 0), b_scan(s, 1)]
            if s + 1 < NTILE:
                gens += [b_pre(s + 1, 0), b_pre(s + 1, 1)]
            run_rr(gens)
        for t in range(NTILE):
            tt('pool', HB[0][:, t, :], HB[0][:, t, :], HB[1][:, t, :], ALU.add, [HB[0], HB[1]], [HB[0]])
            act(JNK2[:], HB[0][:, t, :], AF.Square, [HB[0]], [JNK2, HSS], accum=HSS[:, t:t + 1])
        act(HSS[:], HSS[:], AF.Ln, [HSS, EPSC], [HSS], scale=1.0 / 256, bias=EPSC[:, 0:1])
        act(HSS[:], HSS[:], AF.Exp, [HSS], [HSS], scale=-0.5)
        for t in range(NTILE):
            ts('dve', HB[0][:, t, :], HB[0][:, t, :], HSS[:, t:t + 1], None, ALU.mult, None, [HB[0], HSS], [HB[0]])
            for j in range(2):
                ps = PSM.get()
                tr(ps[:], HB[0][:, t, j * 128:(j + 1) * 128], cf('ident'), [HB[0], CF], [ps])
                stt('dve', yT[:, 8 + 2 * hb + j, t * 128:(t + 1) * 128], ps[:], NAB[:, 1 + j:2 + j],
                    OG[:, j, t * 128:(t + 1) * 128], ALU.mult, ALU.mult, [ps, NAB, OG], [yT])
    fw.scope_end()
    fw.scope_end()

    NH = 512
    fw.scope_begin()
    X1 = fw.sbs("x1", [128, KC, NH], F32)
    XY = fw.sbs("xy", [128, D], F32)
    RV = fw.sbs("rv", [128, NH], F32)
    SQT = [fw.sbs("sqt%d" % i, [128, NH], BF16) for i in range(2)]
    RL = [fw.sbs("rl%d" % i, [128, NH], F32) for i in range(2)]
    shift2 = modT[:, 48:64]

    def rms_from_pn(nfeat):
        act(RV[:], PN[:], AF.Ln, [PN, EPSC], [RV], scale=1.0 / nfeat, bias=EPSC[:, 0:1])
        act(RV[:], RV[:], AF.Exp, [RV], [RV], scale=-0.5)

    for hf in range(2):
        tk = slice(hf * NH, (hf + 1) * NH)
        fw.scope_begin()
        MIX = fw.sbs("mix%d" % hf, [128, KC, NH], F32)
        for cbp in range(KC // 2):
            wb, wv = load_w(w_out[:, cbp * 256:(cbp + 1) * 256], 256)
            for j in range(2):
                cbk = cbp * 2 + j
                pb = PBIG.get()
                for kc in range(KC):
                    mm(pb[:], wv[:, kc, j * 128:(j + 1) * 128], yT[:, kc, tk], kc == 0, kc == KC - 1, [wb, yT], [pb])
                cp('dve', MIX[:, cbk, :], pb[:], [pb], [MIX])
                sq = SQT[cbk % 2]
                act(sq[:], pb[:], AF.Square, [pb], [sq])
                mm(PN[:], ONESB[:], sq[:], cbk == 0, cbk == KC - 1, [ONESB, sq], [PN])
        rms_from_pn(D)
        for c in range(KC):
            tt('dve', MIX[:, c, :], MIX[:, c, :], RV[:], ALU.mult, [MIX, RV], [MIX])
        for t4 in range(4):
            fw.dma('sp', XY[:], xin[hf * NH + t4 * 128:hf * NH + (t4 + 1) * 128, :], w=[XY])
            for c in range(KC):
                ps = PSM.get()
                tr(ps[:], XY[:, c * 128:(c + 1) * 128], cf('ident'), [XY, CF], [ps])
                stt('dve', X1[:, c, t4 * 128:(t4 + 1) * 128], MIX[:, c, t4 * 128:(t4 + 1) * 128], g1p[:, c:c + 1],
                    ps[:], ALU.mult, ALU.add, [MIX, g1p, ps], [X1])
        fw.scope_end()
        fw.scope_begin()
        RT = fw.sbs("rt%d" % hf, [128, 64, NH], BF16)
        fw.scope_begin()
        H2 = fw.sbs("h2%d" % hf, [128, KC, NH], BF16)
        for c in range(KC):
            sq = SQT[c % 2]
            act(sq[:], X1[:, c, :], AF.Square, [X1], [sq])
            mm(PN[:], ONESB[:], sq[:], c == 0, c == KC - 1, [ONESB, sq], [PN])
        rms_from_pn(D)
        for c in range(KC):
            rl = RL[c % 2]
            tt('dve', rl[:], X1[:, c, :], RV[:], ALU.mult, [X1, RV], [rl])
            act(H2[:, c, :], rl[:], AF.Identity, [rl, a2, modT], [H2], scale=a2[:, c:c + 1], bias=shift2[:, c:c + 1])
        for fbp in range(32):
            wb, wv = load_w(w1[:, fbp * 256:(fbp + 1) * 256], 256)
            for j in range(2):
                fb = fbp * 2 + j
                pb = PBIG.get()
                for kc in range(KC):
                    mm(pb[:], wv[:, kc, j * 128:(j + 1) * 128], H2[:, kc, :], kc == 0, kc == KC - 1, [wb, H2], [pb])
                rl = RL[fb % 2]
                act(rl[:], pb[:], AF.Relu, [pb], [rl])
                tt('dve', RT[:, fb, :], rl[:], rl[:], ALU.mult, [rl], [RT])
        fw.scope_end()
        FT = fw.sbs("ft%d" % hf, [128, KC, NH], F32)
        for cbp in range(KC // 2):
            pbs = [PBIG.get(), PBIG.get()]
            for kq in range(4):
                wb, wv = load_w(w2[kq * 2048:(kq + 1) * 2048, cbp * 256:(cbp + 1) * 256], 256)
                for kc in range(KC):
                    kg = kq * KC + kc
                    for j in range(2):
                        mm(pbs[j][:], wv[:, kc, j * 128:(j + 1) * 128], RT[:, kg, :], kg == 0, kg == 63,
                           [wb, RT], [pbs[j]])
            for j in range(2):
                cbk = cbp * 2 + j
                cp('dve', FT[:, cbk, :], pbs[j][:], [pbs[j]], [FT])
                sq = SQT[cbk % 2]
                act(sq[:], pbs[j][:], AF.Square, [pbs[j]], [sq])
                mm(PN[:], ONESB[:], sq[:], cbk == 0, cbk == KC - 1, [ONESB, sq], [PN])
        rms_from_pn(D)
        for c in range(KC):
            tt('dve', FT[:, c, :], FT[:, c, :], RV[:], ALU.mult, [FT, RV], [FT])
            stt('dve', FT[:, c, :], FT[:, c, :], g2p[:, c:c + 1], X1[:, c, :], ALU.mult, ALU.add, [FT, g2p, X1], [FT])
        for t4 in range(4):
            for c in range(KC):
                ps = PSM.get()
                tr(ps[:], FT[:, c, t4 * 128:(t4 + 1) * 128], cf('ident'), [FT, CF], [ps])
                cp('act' if c % 2 else 'dve', XY[:, c * 128:(c + 1) * 128], ps[:], [ps], [XY])
            fw.dma('sp', y[hf * NH + t4 * 128:hf * NH + (t4 + 1) * 128, :], XY[:], r=[XY])
        fw.scope_end()
    fw.scope_end()
    fw.emit()
    return nc, fw


PROMPT_SEQS = {2: [0, 1, 2], 3: [3, 4, 5], 4: [6, 7, 8], 5: [9, 10, 11], 6: [12, 13], 7: [14, 15]}


def make_in_maps(inp):
    cat, offs, sel = make_consts()
    f = np.float32
    shared = {
        'w_ada': np.ascontiguousarray(inp['w_ada'][0]),
        'b_ada': np.ascontiguousarray(inp['b_ada'][0][None, :]),
        'nrm': np.ascontiguousarray(np.stack([inp['norm_mix_pre'][0], inp['norm_mix_post'][0],
                                              inp['norm_ffn_pre'][0], inp['norm_ffn_post'][0]], axis=0)),
        'w_in': np.ascontiguousarray(inp['w_in'][0]),
        'conv_w': np.ascontiguousarray(inp['conv_w'][0]),
        'gparams': np.ascontiguousarray(np.concatenate([inp['a_log'][0].reshape(16), inp['dt_bias'][0].reshape(16),
                                                        inp['mlstm_ibias'][0].reshape(8),
                                                        inp['mlstm_fbias'][0].reshape(8)])[None, :]),
        'norm_a': np.ascontiguousarray(inp['norm_a'].reshape(1, 128)),
        'norm_b': np.ascontiguousarray(inp['norm_b'].reshape(1, 256)),
        'w_out': np.ascontiguousarray(inp['w_out'][0]),
        'w1': np.ascontiguousarray(inp['w_ffn1'][0]),
        'w2': np.ascontiguousarray(inp['w_ffn2'][0]),
        'cst': cat,
    }
    maps = []
    for core in range(NCORES):
        m = dict(shared)
        if core < 2:
            b = core
            m['xin'] = np.ascontiguousarray(inp['x_sample'][b])
            m['cond'] = np.ascontiguousarray(inp['c'][b][None, :])
            m['flags'] = np.ones((128, 2), f)
            m['sdelta'] = np.ascontiguousarray(inp['state_delta'][b, 0])
            m['sC'] = np.ascontiguousarray(inp['state_mlstm_C'][b, 0])
            m['sn'] = np.ascontiguousarray(inp['state_mlstm_n'][b, 0])
            m['sm'] = np.ascontiguousarray(inp['state_mlstm_m'][b, 0].reshape(1, 8))
        else:
            xs = np.zeros((NT, D), f)
            seqs = PROMPT_SEQS[core]
            for s in range(NSLOT):
                xs[s * 256:(s + 1) * 256] = inp['x_prompt'][seqs[s] if s < len(seqs) else seqs[0]]
            m['xin'] = xs
            m['cond'] = np.ascontiguousarray(inp['c_ctx'][None, :])
            m['flags'] = np.zeros((128, 2), f)
            m['sdelta'] = np.zeros((2, 8, 128, 128), f)
            m['sC'] = np.zeros((2, 4, 128, 256), f)
            m['sn'] = np.zeros((2, 4, 128), f)
            m['sm'] = np.zeros((1, 8), f)
        maps.append(m)
    return maps


_CACHE = {}


def kernel(**inputs):
    inp = {k: np.asarray(v) for k, v in inputs.items()}
    maps = make_in_maps(inp)
    if 'nc' not in _CACHE:
        _CACHE['nc'] = build()[0]
    nc = _CACHE['nc']
    res = run_bass_kernel_spmd(nc, maps, core_ids=list(range(NCORES)))
    r = res.results
    f = np.float32
    y_prompt = np.zeros((16, 256, D), f)
    y_sample = np.zeros((2, NT, D), f)
    sd = np.zeros((16, 1, 2, 8, 128, 128), f)
    sCo = np.zeros((16, 1, 2, 4, 128, 256), f)
    sno = np.zeros((16, 1, 2, 4, 128), f)
    smo = np.zeros((16, 1, 2, 4), f)
    for core in range(NCORES):
        o = r[core]
        if core < 2:
            y_sample[core] = o['y']
        else:
            for s, q in enumerate(PROMPT_SEQS[core]):
                y_prompt[q] = o['y'][s * 256:(s + 1) * 256]
                sd[q, 0] = o['st_delta'][s]
                sCo[q, 0] = o['st_C'][s]
                sno[q, 0] = o['st_n'][s]
                smo[q, 0] = o['st_m'][s].reshape(2, 4)
    return (y_prompt, y_sample, sd, sCo, sno, smo)
```
